# Optimizing a Trainium2 kernel written in Bass

```python
import jax, jax.numpy as jnp
from jax import lax
import numpy as np

D_MODEL = 1024
BATCH = 8
SEQ = 8192
DEPTH = 1

MIX_WIDTH = D_MODEL
ATT_WIDTH = MIX_WIDTH // 2
GMLP_WIDTH = MIX_WIDTH - ATT_WIDTH
ATT_HEAD_DIM = 64
N_ATT_HEADS = ATT_WIDTH // ATT_HEAD_DIM
N_GMLP_GROUPS = 8
GMLP_GROUP_DIM = GMLP_WIDTH // N_GMLP_GROUPS
CHUNK = 128
Q_BLOCK = 128
D_FF = 2816
CONV_WIDTH = 3
EPS = 1e-6

Q_END = ATT_WIDTH
K_END = 2 * ATT_WIDTH
V_END = 3 * ATT_WIDTH
U_END = V_END + GMLP_WIDTH
G_END = U_END + GMLP_WIDTH
IN_COLS = G_END + N_ATT_HEADS

kernel_name = "fox_gmlp_convffn_hybrid"


def rmsnorm(x, gain):
    xf = x.astype(jnp.float32)
    y = xf * lax.rsqrt(jnp.mean(xf * xf, axis=-1, keepdims=True) + EPS)
    return y.astype(x.dtype) * gain


def forgetting_attention(q, k, v, log_f):
    B, S, H, Dh = q.shape
    nb = S // Q_BLOCK
    c = jnp.cumsum(log_f, axis=1)
    c_k = c.transpose(0, 2, 1)[:, :, None, :]
    key_pos = jnp.arange(S)
    qb = q.reshape(B, nb, Q_BLOCK, H, Dh).transpose(1, 0, 2, 3, 4)
    cb = c.reshape(B, nb, Q_BLOCK, H).transpose(1, 0, 3, 2)
    pb = key_pos.reshape(nb, Q_BLOCK)
    scale = Dh ** -0.5

    def block(args):
        qi, ci, pi = args
        logits = jnp.einsum('bqhd,bkhd->bhqk', qi, k).astype(jnp.float32) * scale
        logits = logits + ci[..., None] - c_k
        causal = pi[:, None] >= key_pos[None, :]
        logits = jnp.where(causal, logits, -jnp.inf)
        p = jax.nn.softmax(logits, axis=-1).astype(v.dtype)
        return jnp.einsum('bhqk,bkhd->bqhd', p, v)

    out = lax.map(block, (qb, cb, pb))
    return out.transpose(1, 0, 2, 3, 4).reshape(B, S, H * Dh)


def chunked_spatial_gating(u, vg, v_gain, w_s, b_s):
    B, S, G, Dg = u.shape
    nc = S // CHUNK
    vg = rmsnorm(vg, v_gain.reshape(G, Dg))
    vc = vg.reshape(B, nc, CHUNK, G, Dg)
    w = w_s * jnp.tril(jnp.ones((CHUNK, CHUNK), dtype=w_s.dtype))
    mixed = jnp.einsum('gij,bnjgd->bnigd', w, vc) + b_s.T[None, None, :, :, None]
    return (u * mixed.reshape(B, S, G, Dg)).reshape(B, S, G * Dg)


def conv_ffn(x, w_up, conv_w, conv_b, w_down):
    h = x @ w_up
    h = lax.conv_general_dilated(
        h, conv_w[:, None, :], window_strides=(1,),
        padding=[(CONV_WIDTH - 1, 0)],
        dimension_numbers=('NWC', 'WIO', 'NWC'),
        feature_group_count=h.shape[-1]) + conv_b
    a, g = jnp.split(h, 2, axis=-1)
    return (jax.nn.silu(g) * a) @ w_down


def setup_inputs(seed: int = 0) -> dict:
    key = jax.random.key(seed)
    ks = jax.random.split(key, 16)
    f32 = jnp.float32
    x = jax.random.normal(ks[0], (BATCH, SEQ, D_MODEL), f32)
    norm_mix_g = 1.0 + 0.02 * jax.random.normal(ks[1], (DEPTH, D_MODEL), f32)
    w_in = jax.random.normal(ks[2], (DEPTH, D_MODEL, IN_COLS), f32) * D_MODEL ** -0.5
    b_forget = 2.0 + 0.5 * jax.random.normal(ks[3], (DEPTH, N_ATT_HEADS), f32)
    gmlp_norm_g = 1.0 + 0.02 * jax.random.normal(ks[4], (DEPTH, GMLP_WIDTH), f32)
    w_spatial = jax.random.normal(ks[5], (DEPTH, N_GMLP_GROUPS, CHUNK, CHUNK), f32) * CHUNK ** -0.5
    b_spatial = 1.0 + 0.1 * jax.random.normal(ks[6], (DEPTH, N_GMLP_GROUPS, CHUNK), f32)
    w_out = jax.random.normal(ks[7], (DEPTH, MIX_WIDTH, D_MODEL), f32) * MIX_WIDTH ** -0.5
    norm_ffn_g = 1.0 + 0.02 * jax.random.normal(ks[8], (DEPTH, D_MODEL), f32)
    w_up = jax.random.normal(ks[9], (DEPTH, D_MODEL, 2 * D_FF), f32) * D_MODEL ** -0.5
    conv_w = jax.random.normal(ks[10], (DEPTH, CONV_WIDTH, 2 * D_FF), f32) * CONV_WIDTH ** -0.5
    conv_b = 0.01 * jax.random.normal(ks[11], (DEPTH, 2 * D_FF), f32)
    w_down = jax.random.normal(ks[12], (DEPTH, D_FF, D_MODEL), f32) * D_FF ** -0.5
    norm_final_g = 1.0 + 0.02 * jax.random.normal(ks[13], (D_MODEL,), f32)
    return {"x": x, "norm_mix_g": norm_mix_g, "w_in": w_in, "b_forget": b_forget,
            "gmlp_norm_g": gmlp_norm_g, "w_spatial": w_spatial, "b_spatial": b_spatial,
            "w_out": w_out, "norm_ffn_g": norm_ffn_g, "w_up": w_up, "conv_w": conv_w,
            "conv_b": conv_b, "w_down": w_down, "norm_final_g": norm_final_g}


def reference(x, norm_mix_g, w_in, b_forget, gmlp_norm_g, w_spatial, b_spatial,
              w_out, norm_ffn_g, w_up, conv_w, conv_b, w_down, norm_final_g):
    B, S, _ = x.shape
    h = x
    for layer in range(DEPTH):
        xn = rmsnorm(h, norm_mix_g[layer])
        proj = xn @ w_in[layer]
        q = proj[..., :Q_END].reshape(B, S, N_ATT_HEADS, ATT_HEAD_DIM)
        k = proj[..., Q_END:K_END].reshape(B, S, N_ATT_HEADS, ATT_HEAD_DIM)
        v = proj[..., K_END:V_END].reshape(B, S, N_ATT_HEADS, ATT_HEAD_DIM)
        log_f = jax.nn.log_sigmoid(
            (proj[..., G_END:] + b_forget[layer]).astype(jnp.float32))
        att = forgetting_attention(q, k, v, log_f)
        uv = jax.nn.gelu(proj[..., V_END:G_END])
        u = uv[..., :GMLP_WIDTH].reshape(B, S, N_GMLP_GROUPS, GMLP_GROUP_DIM)
        vg = uv[..., GMLP_WIDTH:].reshape(B, S, N_GMLP_GROUPS, GMLP_GROUP_DIM)
        sg = chunked_spatial_gating(u, vg, gmlp_norm_g[layer], w_spatial[layer], b_spatial[layer])
        mix = jnp.concatenate([att, sg], axis=-1)
        h = h + mix @ w_out[layer]
        h = h + conv_ffn(rmsnorm(h, norm_ffn_g[layer]), w_up[layer], conv_w[layer],
                         conv_b[layer], w_down[layer])
    return rmsnorm(h, norm_final_g)
```

```python
import numpy as np
from contextlib import ExitStack
import concourse.bass as bass
import concourse.mybir as mybir
from concourse.bass_utils import run_bass_kernel_spmd

F32 = mybir.dt.float32
BF16 = mybir.dt.bfloat16
AF = mybir.ActivationFunctionType
ALU = mybir.AluOpType
AX = mybir.AxisListType

D = 1024
KC = 8
H = 8
DFF = 2816
NPAIR = DFF // 128
IN_COLS = 2568
EPS = 1e-6
TQ = 512
CH = 16
NSB = 3
ENG_NAMES = ["pe", "act", "dve", "pool", "sp"]


class Res:
    def __init__(self):
        self.w = None
        self.r = []


class Prog:
    def __init__(self, nc):
        self.nc = nc
        self.ops = {e: [] for e in ENG_NAMES}
        self.cnt = {}
        self.sem_keys = []

    def _bump(self, key, n):
        if key not in self.cnt:
            self.cnt[key] = 0
            self.sem_keys.append(key)
        self.cnt[key] += n
        return (key, self.cnt[key])

    def _deps(self, reads, writes, extra):
        w = [t for t in extra if t is not None]
        for r in reads:
            if r.w is not None:
                w.append(r.w)
        for r in writes:
            if r.w is not None:
                w.append(r.w)
            w.extend(r.r)
        return w

    def _done(self, tok, reads, writes):
        for r in reads:
            r.r.append(tok)
        for r in writes:
            r.w = tok
            r.r = []

    def op(self, eng, fn, reads=(), writes=(), extra=()):
        waits = self._deps(reads, writes, extra)
        tok = self._bump(eng, 1)
        self.ops[eng].append((fn, waits, (eng, 1)))
        self._done(tok, reads, writes)
        return tok

    def group(self, eng, fns, reads=(), writes=(), extra=()):
        waits = self._deps(reads, writes, extra)
        for i, fn in enumerate(fns):
            last = i == len(fns) - 1
            if last:
                tok = self._bump(eng, 1)
            self.ops[eng].append((fn, waits if i == 0 else [], (eng, 1) if last else None))
        self._done(tok, reads, writes)
        return tok

    def dma(self, eng, semkey, fn, reads=(), writes=(), extra=()):
        waits = self._deps(reads, writes, extra)
        tok = self._bump(semkey, 16)
        self.ops[eng].append((fn, waits, (semkey, 16)))
        self._done(tok, reads, writes)
        return tok

    def emit(self, final_waits=(), gate_wait=None, gate_inc=None):
        nc = self.nc
        with ExitStack() as es:
            sems = {}
            for k in self.sem_keys:
                sems[k] = es.enter_context(nc.semaphore("s_" + k))
            block = es.enter_context(nc.Block())
            engs = {"pe": block.tensor, "act": block.scalar, "dve": block.vector,
                    "pool": block.gpsimd, "sp": block.sync}
            for en in ENG_NAMES:
                ops = self.ops[en]
                fw = [t for t in final_waits if t is not None] if en == "sp" else []
                if not ops and not fw:
                    continue

                def body(e, ops=ops, fw=fw, en=en):
                    seen = {}
                    if gate_wait is not None:
                        e.wait_ge(gate_wait[0], gate_wait[1])
                    for fn, waits, inc in ops:
                        for (k, v) in waits:
                            if seen.get(k, 0) < v:
                                e.wait_ge(sems[k], v)
                                seen[k] = v
                        ins = fn(e)
                        if inc is not None:
                            ins.then_inc(sems[inc[0]], inc[1])
                    for (k, v) in fw:
                        if seen.get(k, 0) < v:
                            e.wait_ge(sems[k], v)
                            seen[k] = v
                    if gate_inc is not None and en == "sp":
                        e.sem_inc(gate_inc, 1)
                engs[en](body)


class Rot:
    def __init__(self, items):
        self.items = [(a, Res()) for a in items]
        self.i = 0

    def next(self):
        it = self.items[self.i % len(self.items)]
        self.i += 1
        return it


def _norm_transpose(P, xt, r_xt, gbc, r_gbc, xnb, xnT, r_xnT, trb, r_trb, idn, r_idn, cst, r_cst, junk, r_junk, st):
    ssq, lnv, rstd, r_cols = st
    for b in range(4):
        rc_ = r_cols[b]
        P.op("act", lambda e, b=b: e.activation(out=junk[:], in_=xt[:, b, :], func=AF.Square,
                                                accum_out=ssq[:, b:b + 1]),
             reads=[r_xt[b]], writes=[r_junk, rc_])
        P.op("act", lambda e, b=b: e.activation(out=lnv[:, b:b + 1], in_=ssq[:, b:b + 1], func=AF.Ln, bias=cst[:, 0:1],
                                                scale=1.0 / D), reads=[r_cst], writes=[rc_])
        P.op("act", lambda e, b=b: e.activation(out=rstd[:, b:b + 1], in_=lnv[:, b:b + 1], func=AF.Exp, scale=-0.5),
             writes=[rc_])
    for b in range(4):
        rc_ = r_cols[b]
        xb, r_xb = xnb.next()
        P.op("dve", lambda e, b=b, xb=xb: e.tensor_scalar(out=xb[:], in0=xt[:, b, :], scalar1=rstd[:, b:b + 1],
                                                          scalar2=None, op0=ALU.mult),
             reads=[r_xt[b], rc_], writes=[r_xb])
        P.group("pe", [lambda e, kc=kc, xb=xb: e.transpose(trb[:, kc * 128:(kc + 1) * 128],
                                                            xb[:, kc * 128:(kc + 1) * 128], idn[:])
                       for kc in range(KC)], reads=[r_xb, r_idn], writes=[r_trb])
        P.op("dve", lambda e, b=b: e.tensor_tensor(out=xnT[:, :, b * 128:(b + 1) * 128],
                                                   in0=trb[:].rearrange("p (k t) -> p k t", t=128),
                                                   in1=gbc[:, :].unsqueeze(2).broadcast_to([128, KC, 128]), op=ALU.mult),
             reads=[r_trb, r_gbc], writes=[r_xnT])


def build_nc(S=8192, only_pass1=False):
    assert S % TQ == 0
    NT = S // TQ
    NBLK = S // 128
    nc = bass.Bass("TRN2", target_bir_lowering=False)

    def din(name, shape):
        return nc.dram_tensor(name, shape, F32, kind="ExternalInput").ap()

    x_d = din("x", [S, D])
    g1_d = din("norm_mix_g", [D])
    win_d = din("w_in", [D, IN_COLS])
    bf_d = din("b_forget", [H])
    gg_d = din("gmlp_norm_g", [512])
    wsp_d = din("w_spatial", [8, 128, 128])
    bsp_d = din("b_spatial", [8, 128])
    wout_d = din("w_out", [D, D])
    g2_d = din("norm_ffn_g", [D])
    wup_d = din("w_up", [D, 2 * DFF])
    cw_d = din("conv_w", [3, 2 * DFF])
    cb_d = din("conv_b", [2 * DFF])
    wdn_d = din("w_down", [DFF, D])
    g3_d = din("norm_final_g", [D])
    out_d = nc.dram_tensor("out", [S, D], F32, kind="ExternalOutput").ap()
    kT_d = nc.dram_tensor("kT_scr", [H, 64, S], BF16, kind="Internal").ap()
    v_d = nc.dram_tensor("v_scr", [128, H, NBLK, 128], BF16, kind="Internal").ap()
    wupb_d = nc.dram_tensor("wup_scr", [D, 2 * DFF], BF16, kind="Internal").ap()
    wdnb_d = nc.dram_tensor("wdn_scr", [DFF, D], BF16, kind="Internal").ap()
    cwt_d = nc.dram_tensor("cwt_scr", [128, 2 * NPAIR, 4], F32, kind="Internal").ap()

    h1_tokens = []
    gate_cm = nc.semaphore("gate")
    gate = gate_cm.__enter__()

    with ExitStack() as es:
        def sb(name, shape, dt):
            return es.enter_context(nc.sbuf_tensor(name, shape, dt))

        def ps(name, shape, dt):
            return es.enter_context(nc.psum_tensor(name, shape, dt))

        P = Prog(nc)
        win = sb("win", [128, KC, IN_COLS], BF16); r_win = Res()
        woa = sb("woa", [128, 4, D], BF16); r_woa = Res()
        wos = sb("wos", [128, 4, D], BF16); r_wos = Res()
        xs = Rot([sb(f"xs{i}", [128, D], F32) for i in range(2)])
        xo = Rot([sb(f"xo{i}", [128, D], F32) for i in range(2)])
        xnb = Rot([sb(f"xnb{i}", [128, D], BF16) for i in range(2)])
        xnT = sb("xnT", [128, KC, TQ], BF16); r_xnT = Res()
        Qas = [(sb(f"Qa{i}", [65, H, TQ], BF16), Res()) for i in range(2)]
        kst = sb("kst", [64, H, TQ], BF16); r_kst = Res()
        vst = sb("vst", [128, H, 4, 128], BF16); r_vst = Res()
        ksl = Rot([sb(f"ksl{i}", [65, CH * 128], BF16) for i in range(NSB)])
        vsl = Rot([sb(f"vsl{i}", [128, CH, 128], BF16) for i in range(NSB)])
        Pt = Rot([sb(f"Pt{i}", [128, TQ], BF16) for i in range(3)])
        attTs = [(sb(f"attT{i}", [128, 4, TQ], BF16), Res()) for i in range(2)]
        uT = sb("uT", [128, 4, TQ], BF16); r_uT = Res()
        sgTs = [(sb(f"sgT{i}", [128, 4, TQ], BF16), Res()) for i in range(2)]
        vgf = Rot([sb(f"vgf{i}", [128, 512], F32) for i in range(2)])
        sqs = [(sb(f"sq{i}", [128, 512], F32), Res()) for i in range(2)]
        vgn = sb("vgn", [128, 4, 512], BF16); r_vgn = Res()
        gtmps = [(sb(f"gtmp{i}", [128, TQ], F32), Res()) for i in range(2)]
        gbc = sb("gbc", [128, KC], F32); r_gbc = Res()
        gnb = sb("gnb", [128, 512], F32); r_gnb = Res()
        idn = sb("idn", [128, 128], BF16); r_idn = Res()
        onesf = sb("onesf", [128, 128], F32); r_onesf = Res()
        negf = sb("negf", [128, 128], F32); r_negf = Res()
        utri = sb("utri", [128, 128], F32); r_utri = Res()
        mneg = sb("mneg", [128, 128], BF16); r_mneg = Res()
        wspf = sb("wspf", [128, 128], F32); r_wspf = Res()
        wspb = sb("wspb", [128, 128], BF16); r_wspb = Res()
        wspT = sb("wspT", [128, 8, 128], BF16); r_wspT = Res()
        bS = sb("bS", [128, 4, 128], F32); r_bS = Res()
        bfb = sb("bfb", [128, H], F32); r_bfb = Res()
        cst = sb("cst", [128, 2], F32); r_cst = Res()
        stt = (sb("ssq", [128, 4], F32), sb("lnv", [128, 4], F32), sb("rstd", [128, 4], F32), Res())
        gsss = [(sb(f"gss{i}", [128, 8], F32), Res()) for i in range(2)]
        zf = sb("zf", [128, 4, H], F32); r_zf = Res()
        nl = sb("nl", [128, 4, H], F32); r_nl = Res()
        Aall = sb("Aall", [128, NBLK, H], F32); r_Aall = Res()
        carryB = sb("carryB", [128, NBLK + 1, H], F32); r_carry = Res()
        nbs = [(sb(f"nb{i}", [128, NBLK, H], F32), Res()) for i in range(1)] * 2

        trb = ps("trb", [128, 1024], BF16); r_trb = Res()
        pbs = [ps(f"pb{i}", [128, 512], F32) for i in range(7)]
        Ab = Rot(pbs[0:2])
        Sb = Rot(pbs[2:5])
        Ob = Rot(pbs[5:7])

        P.op("pool", lambda e: e.memset(onesf[:], 1.0), writes=[r_onesf])
        P.op("pool", lambda e: e.memset(negf[:], -30000.0), writes=[r_negf])
        P.op("pool", lambda e: e.memset(cst[:, 0:1], EPS), writes=[r_cst])
        P.op("pool", lambda e: e.memset(cst[:, 1:2], 1.0), writes=[r_cst])
        P.op("pool", lambda e: e.affine_select(out=idn[:], in_=onesf[:], pattern=[[-1, 128]], compare_op=ALU.is_equal,
                                               fill=0.0, base=0, channel_multiplier=1), reads=[r_onesf], writes=[r_idn])
        P.op("pool", lambda e: e.affine_select(out=utri[:], in_=onesf[:], pattern=[[1, 128]], compare_op=ALU.is_ge,
                                               fill=0.0, base=0, channel_multiplier=-1), reads=[r_onesf], writes=[r_utri])
        P.op("pool", lambda e: e.affine_select(out=mneg[:], in_=negf[:], pattern=[[-1, 128]], compare_op=ALU.is_gt,
                                               fill=0.0, base=0, channel_multiplier=1), reads=[r_negf], writes=[r_mneg])
        P.op("pool", lambda e: e.memset(carryB[:, 0, :], 0.0), writes=[r_carry])
        P.op("pool", lambda e: e.memset(vst[:, :, :, 64:128], 1.0), writes=[r_vst])
        for (a, r) in ksl.items:
            P.op("pool", lambda e, a=a: e.memset(a[64:65, :], 1.0), writes=[r])
        P.dma("sp", "c_gbc", lambda e: e.dma_start(out=gbc[:], in_=g1_d.rearrange("(k p) -> p k", p=128), allow_slow_non_contiguous=True), writes=[r_gbc])
        P.dma("sp", "c_gnb", lambda e: e.dma_start(out=gnb[:], in_=gg_d.partition_broadcast(128)), writes=[r_gnb])
        P.dma("sp", "c_bfb", lambda e: e.dma_start(out=bfb[:], in_=bf_d.partition_broadcast(128)), writes=[r_bfb])
        for g in range(8):
            gp, gg2 = g // 2, g % 2
            P.dma("sp", "c_bS", lambda e, g=g, gp=gp, gg2=gg2: e.dma_start(
                out=bS[gg2 * 64:(gg2 + 1) * 64, gp, :], in_=bsp_d[g].partition_broadcast(64)), writes=[r_bS])
        win_src = win_d.rearrange("(k p) n -> p k n", p=128)
        r_winq = {g: Res() for g in "KVQFUG"}
        for g, (c0, c1) in (("K", (512, 1024)), ("V", (1024, 1536)), ("Q", (0, 512)), ("F", (2560, 2568)),
                            ("U", (1536, 2048)), ("G", (2048, 2560))):
            P.dma("pool", "w_in" + g, lambda e, c0=c0, c1=c1: e.dma_start(out=win[:, :, c0:c1], in_=win_src[:, :, c0:c1],
                                                                        max_dma_last_dim=4096), writes=[r_winq[g]])
        P.dma("pool", "w_oa", lambda e: e.dma_start(out=woa[:], in_=wout_d[0:512, :].rearrange("(c p) n -> p c n", p=128),
                                                    max_dma_last_dim=4096), writes=[r_woa])
        P.dma("pool", "w_os", lambda e: e.dma_start(out=wos[:], in_=wout_d[512:1024, :].rearrange("(c p) n -> p c n", p=128),
                                                    max_dma_last_dim=4096), writes=[r_wos])
        for g in range(8):
            P.dma("sp", "c_wsp", lambda e, g=g: e.dma_start(out=wspf[:], in_=wsp_d[g]), writes=[r_wspf])
            P.op("pool", lambda e: e.affine_select(out=wspb[:], in_=wspf[:], pattern=[[-1, 128]], compare_op=ALU.is_ge,
                                                   fill=0.0, base=0, channel_multiplier=1), reads=[r_wspf], writes=[r_wspb])
            P.op("pe", lambda e: e.transpose(trb[:, 0:128], wspb[:], idn[:]), reads=[r_wspb, r_idn], writes=[r_trb])
            P.op("dve", lambda e, g=g: e.tensor_copy(out=wspT[:, g, :], in_=trb[:, 0:128]), reads=[r_trb], writes=[r_wspT])
        kw_toks, vw_toks = {}, {}
        r_cols = [Res() for _ in range(4)]
        ssq, lnv, rstd, _ = stt

        def pre(T):
            n = 4 * T + 4
            Qa, r_Qa = Qas[T % 2]
            nb, r_nb = nbs[T % 2]
            sgT, r_sgT = sgTs[T % 2]
            for b in range(4):
                xs_, r_xs = xs.next()
                P.dma("sp", f"ld_x{(xs.i - 1) % 2}", lambda e, b=b, xs_=xs_: e.dma_start(
                    out=xs_[:], in_=x_d[T * TQ + b * 128:T * TQ + (b + 1) * 128, :]), writes=[r_xs])
                yield 6
                rc_ = r_cols[b]
                xb, r_xb = xnb.next()
                P.op("act", lambda e, b=b, xs_=xs_, xb=xb: e.activation(out=xb[:], in_=xs_[:], func=AF.Square,
                                                                        accum_out=ssq[:, b:b + 1]), reads=[r_xs], writes=[r_xb, rc_])
                yield 2
                P.op("act", lambda e, b=b: e.activation(out=lnv[:, b:b + 1], in_=ssq[:, b:b + 1], func=AF.Ln,
                                                        bias=cst[:, 0:1], scale=1.0 / D), reads=[r_cst], writes=[rc_])
                yield 1
                P.op("act", lambda e, b=b: e.activation(out=rstd[:, b:b + 1], in_=lnv[:, b:b + 1], func=AF.Exp, scale=-0.5),
                     writes=[rc_])
                yield 2
                P.op("dve", lambda e, b=b, xb=xb, xs_=xs_: e.tensor_scalar(out=xb[:], in0=xs_[:], scalar1=rstd[:, b:b + 1],
                                                                           scalar2=None, op0=ALU.mult),
                     reads=[r_xs, rc_], writes=[r_xb])
                yield 3
                P.group("pe", [lambda e, kc=kc, xb=xb: e.transpose(trb[:, kc * 128:(kc + 1) * 128],
                                                                    xb[:, kc * 128:(kc + 1) * 128], idn[:])
                               for kc in range(KC)], reads=[r_xb, r_idn], writes=[r_trb])
                yield 3
                P.op("dve", lambda e, b=b: e.tensor_tensor(out=xnT[:, :, b * 128:(b + 1) * 128],
                                                           in0=trb[:].rearrange("p (k t) -> p k t", t=128),
                                                           in1=gbc[:, :].unsqueeze(2).broadcast_to([128, KC, 128]), op=ALU.mult),
                     reads=[r_trb, r_gbc], writes=[r_xnT])
                yield 2
            for hp0 in (0, 2):
                pas = []
                for hp in (hp0, hp0 + 1):
                    pa, r_pa = Ab.next()
                    P.group("pe", [lambda e, kc=kc, hp=hp, pa=pa: e.matmul(
                        pa[:], lhsT=win[:, kc, 512 + hp * 128:512 + (hp + 1) * 128], rhs=xnT[:, kc, :],
                        start=(kc == 0), stop=(kc == KC - 1)) for kc in range(KC)], reads=[r_winq["K"], r_xnT], writes=[r_pa])
                    pas.append((hp, pa, r_pa))
                yield 5
                for hp, pa, r_pa in pas:
                    P.op("dve", lambda e, hp=hp, pa=pa: e.tensor_copy(out=kst[0:64, 2 * hp, :], in_=pa[0:64, :]),
                         reads=[r_pa], writes=[r_kst])
                    P.op("dve", lambda e, hp=hp, pa=pa: e.tensor_copy(out=kst[0:64, 2 * hp + 1, :], in_=pa[64:128, :]),
                         reads=[r_pa], writes=[r_kst])
                yield 2
            kw_toks[T] = P.dma("sp", "st_k", lambda e: e.dma_start(
                out=kT_d[:, :, T * TQ:(T + 1) * TQ].rearrange("h d t -> d h t"), in_=kst[:]), reads=[r_kst])
            for b0 in (0, 2):
                pas = []
                for b in (b0, b0 + 1):
                    pa, r_pa = Ab.next()
                    P.group("pe", [lambda e, kc=kc, b=b, pa=pa: e.matmul(
                        pa[:], lhsT=xnT[:, kc, b * 128:(b + 1) * 128], rhs=win[:, kc, 1024:1536],
                        start=(kc == 0), stop=(kc == KC - 1)) for kc in range(KC)], reads=[r_winq["V"], r_xnT], writes=[r_pa])
                    pas.append((b, pa, r_pa))
                yield 5
                for b, pa, r_pa in pas:
                    P.op("dve", lambda e, b=b, pa=pa: e.tensor_copy(
                        out=vst[:, :, b, 0:64], in_=pa[:].rearrange("p (h d) -> p h d", d=64)), reads=[r_pa], writes=[r_vst])
                yield 2
            vw_toks[T] = P.dma("sp", "st_v", lambda e: e.dma_start(out=v_d[:, :, 4 * T:4 * T + 4, :], in_=vst[:]),
                               reads=[r_vst])
            for hp0 in (0, 2):
                pas = []
                for hp in (hp0, hp0 + 1):
                    pa, r_pa = Ab.next()
                    P.group("pe", [lambda e, kc=kc, hp=hp, pa=pa: e.matmul(
                        pa[:], lhsT=win[:, kc, hp * 128:(hp + 1) * 128], rhs=xnT[:, kc, :],
                        start=(kc == 0), stop=(kc == KC - 1)) for kc in range(KC)], reads=[r_winq["Q"], r_xnT], writes=[r_pa])
                    pas.append((hp, pa, r_pa))
                yield 5
                for hp, pa, r_pa in pas:
                    P.op("dve", lambda e, hp=hp, pa=pa: e.tensor_scalar(out=Qa[0:64, 2 * hp, :], in0=pa[0:64, :],
                                                                        scalar1=0.125, scalar2=None, op0=ALU.mult),
                         reads=[r_pa], writes=[r_Qa])
                    P.op("dve", lambda e, hp=hp, pa=pa: e.tensor_scalar(out=Qa[0:64, 2 * hp + 1, :], in0=pa[64:128, :],
                                                                        scalar1=0.125, scalar2=None, op0=ALU.mult),
                         reads=[r_pa], writes=[r_Qa])
                yield 2
            pa, r_pa = Ab.next()
            for b in range(4):
                P.group("pe", [lambda e, kc=kc, b=b, pa=pa: e.matmul(
                    pa[:, b * 8:(b + 1) * 8], lhsT=xnT[:, kc, b * 128:(b + 1) * 128], rhs=win[:, kc, 2560:2568],
                    start=(kc == 0), stop=(kc == KC - 1)) for kc in range(KC)], reads=[r_winq["F"], r_xnT], writes=[r_pa])
            yield 4
            P.op("dve", lambda e, pa=pa: e.tensor_tensor(
                out=zf[:], in0=pa[:, 0:32].rearrange("p (b h) -> p b h", h=H),
                in1=bfb[:].unsqueeze(1).broadcast_to([128, 4, H]), op=ALU.add), reads=[r_pa, r_bfb], writes=[r_zf])
            yield 3
            P.op("act", lambda e: e.activation(out=zf[:], in_=zf[:], func=AF.Exp, scale=-1.0), writes=[r_zf])
            yield 2
            P.op("act", lambda e: e.activation(out=nl[:], in_=zf[:], func=AF.Ln, bias=cst[:, 1:2], scale=1.0),
                 reads=[r_zf, r_cst], writes=[r_nl])
            yield 2
            pa, r_pa = Ab.next()
            for b in range(4):
                P.op("pe", lambda e, b=b, pa=pa: e.matmul(pa[:, b * 8:(b + 1) * 8], lhsT=utri[:], rhs=nl[:, b, :],
                                                          start=True, stop=True), reads=[r_utri, r_nl], writes=[r_pa])
                P.op("pe", lambda e, b=b, pa=pa: e.matmul(pa[:, 32 + b * 8:32 + (b + 1) * 8], lhsT=onesf[:], rhs=nl[:, b, :],
                                                          start=True, stop=True), reads=[r_onesf, r_nl], writes=[r_pa])
            yield 4
            for b in range(4):
                blk = 4 * T + b
                P.op("dve", lambda e, b=b, blk=blk, pa=pa: e.tensor_tensor(
                    out=Aall[:, blk, :], in0=pa[:, b * 8:(b + 1) * 8], in1=carryB[:, blk, :], op=ALU.add),
                    reads=[r_pa, r_carry], writes=[r_Aall])
                P.op("dve", lambda e, b=b, blk=blk, pa=pa: e.tensor_tensor(
                    out=carryB[:, blk + 1, :], in0=pa[:, 32 + b * 8:32 + (b + 1) * 8], in1=carryB[:, blk, :], op=ALU.add),
                    reads=[r_pa], writes=[r_carry])
                yield 1
            P.op("dve", lambda e: e.tensor_tensor(
                out=Qa[64:65, :, :].rearrange("p h (b t) -> p h b t", t=128),
                in0=carryB[64:65, 4 * T, :].unsqueeze(2).unsqueeze(3).broadcast_to([1, H, 4, 128]),
                in1=carryB[64:65, 4 * T:4 * T + 4, :].rearrange("p b h -> p h b").unsqueeze(3).broadcast_to([1, H, 4, 128]),
                op=ALU.subtract), reads=[r_carry], writes=[r_Qa])
            yield 1
            for gp0 in (0, 2):
                pas = []
                for gp in (gp0, gp0 + 1):
                    pa, r_pa = Ab.next()
                    P.group("pe", [lambda e, kc=kc, gp=gp, pa=pa: e.matmul(
                        pa[:], lhsT=win[:, kc, 1536 + gp * 128:1536 + (gp + 1) * 128], rhs=xnT[:, kc, :],
                        start=(kc == 0), stop=(kc == KC - 1)) for kc in range(KC)], reads=[r_winq["U"], r_xnT], writes=[r_pa])
                    pas.append((gp, pa, r_pa))
                yield 5
                for gp, pa, r_pa in pas:
                    P.op("act", lambda e, gp=gp, pa=pa: e.activation(out=uT[:, gp, :], in_=pa[:], func=AF.Gelu_apprx_tanh),
                         reads=[r_pa], writes=[r_uT])
                yield 2
            for b0 in (0, 2):
                st_ = []
                for k_, b in enumerate((b0, b0 + 1)):
                    pa, r_pa = Ab.next()
                    P.group("pe", [lambda e, kc=kc, b=b, pa=pa: e.matmul(
                        pa[:], lhsT=xnT[:, kc, b * 128:(b + 1) * 128], rhs=win[:, kc, 2048:2560],
                        start=(kc == 0), stop=(kc == KC - 1)) for kc in range(KC)], reads=[r_winq["G"], r_xnT], writes=[r_pa])
                    vf, r_vf = vgf.next()
                    st_.append((b, pa, r_pa, vf, r_vf, sqs[k_][0], sqs[k_][1], gsss[k_][0], gsss[k_][1]))
                yield 5
                for (b, pa, r_pa, vf, r_vf, sq_, r_sq_, gs_, r_gs_) in st_:
                    P.op("act", lambda e, pa=pa, vf=vf: e.activation(out=vf[:], in_=pa[:], func=AF.Gelu_apprx_tanh),
                         reads=[r_pa], writes=[r_vf])
                yield 3
                for (b, pa, r_pa, vf, r_vf, sq_, r_sq_, gs_, r_gs_) in st_:
                    P.op("dve", lambda e, vf=vf, sq_=sq_: e.tensor_tensor(out=sq_[:], in0=vf[:], in1=vf[:], op=ALU.mult),
                         reads=[r_vf], writes=[r_sq_])
                yield 1
                for (b, pa, r_pa, vf, r_vf, sq_, r_sq_, gs_, r_gs_) in st_:
                    P.op("dve", lambda e, sq_=sq_, gs_=gs_: e.tensor_reduce(
                        out=gs_[:, 0:8], in_=sq_[:].rearrange("p (g d) -> p g d", d=64), axis=AX.X, op=ALU.add),
                        reads=[r_sq_], writes=[r_gs_])
                yield 3
                for (b, pa, r_pa, vf, r_vf, sq_, r_sq_, gs_, r_gs_) in st_:
                    P.op("act", lambda e, gs_=gs_: e.activation(out=gs_[:, 0:8], in_=gs_[:, 0:8], func=AF.Ln, bias=cst[:, 0:1],
                                                                scale=1.0 / 64), reads=[r_cst], writes=[r_gs_])
                yield 2
                for (b, pa, r_pa, vf, r_vf, sq_, r_sq_, gs_, r_gs_) in st_:
                    P.op("act", lambda e, gs_=gs_: e.activation(out=gs_[:, 0:8], in_=gs_[:, 0:8], func=AF.Exp, scale=-0.5),
                         writes=[r_gs_])
                yield 3
                for (b, pa, r_pa, vf, r_vf, sq_, r_sq_, gs_, r_gs_) in st_:
                    P.op("dve", lambda e, vf=vf, sq_=sq_, gs_=gs_: e.tensor_tensor(
                        out=sq_[:].rearrange("p (g d) -> p g d", d=64), in0=vf[:].rearrange("p (g d) -> p g d", d=64),
                        in1=gs_[:, 0:8].unsqueeze(2).broadcast_to([128, 8, 64]), op=ALU.mult),
                        reads=[r_vf, r_gs_], writes=[r_sq_])
                yield 1
                for (b, pa, r_pa, vf, r_vf, sq_, r_sq_, gs_, r_gs_) in st_:
                    P.op("dve", lambda e, b=b, sq_=sq_: e.tensor_tensor(out=vgn[:, b, :], in0=sq_[:], in1=gnb[:], op=ALU.mult),
                         reads=[r_sq_, r_gnb], writes=[r_vgn])
                yield 1
            for gp0 in (0, 2):
                pas = []
                for gp in (gp0, gp0 + 1):
                    pa, r_pa = Ab.next()
                    fns = []
                    for b in range(4):
                        for gg2 in range(2):
                            g = 2 * gp + gg2
                            fns.append(lambda e, b=b, gg2=gg2, g=g, pa=pa: e.matmul(
                                pa[gg2 * 64:(gg2 + 1) * 64, b * 128:(b + 1) * 128], lhsT=vgn[:, b, g * 64:(g + 1) * 64],
                                rhs=wspT[:, g, :], start=True, stop=True))
                    P.group("pe", fns, reads=[r_vgn, r_wspT], writes=[r_pa])
                    pas.append((gp, pa, r_pa))
                yield 4
                for k_, (gp, pa, r_pa) in enumerate(pas):
                    gt_, r_gt_ = gtmps[k_]
                    P.op("dve", lambda e, gp=gp, pa=pa, gt_=gt_: e.tensor_tensor(
                        out=gt_[:].rearrange("p (b t) -> p b t", t=128), in0=pa[:].rearrange("p (b t) -> p b t", t=128),
                        in1=bS[:, gp, :].unsqueeze(1).broadcast_to([128, 4, 128]), op=ALU.add),
                        reads=[r_pa, r_bS], writes=[r_gt_])
                yield 1
                for k_, (gp, pa, r_pa) in enumerate(pas):
                    gt_, r_gt_ = gtmps[k_]
                    P.op("dve", lambda e, gp=gp, gt_=gt_: e.tensor_tensor(out=sgT[:, gp, :], in0=gt_[:], in1=uT[:, gp, :], op=ALU.mult),
                         reads=[r_gt_, r_uT], writes=[r_sgT])
                yield 1

        def advance(gen, k):
            if gen is None:
                return
            for _ in range(k):
                try:
                    next(gen)
                except StopIteration:
                    return

        N_UNITS = 37

        def att_setup(T):
            n = 4 * T + 4
            Qa, r_Qa = Qas[T % 2]
            nb, r_nb = nbs[T % 2]
            kw_tok, vw_tok = kw_toks[T], vw_toks[T]
            P.op("dve", lambda e: e.tensor_tensor(
                out=nb[:, 0:n, :], in0=Aall[:, 0:n, :],
                in1=carryB[:, 4 * T, :].unsqueeze(1).broadcast_to([128, n, H]), op=ALU.subtract),
                reads=[r_Aall, r_carry], writes=[r_nb])
            items = []
            chunks = []
            for h in range(H):
                for c0 in range(0, n, CH):
                    c1 = min(n, c0 + CH)
                    chunks.append([h, c0, c1, None])
                    for j in range(c0, c1):
                        items.append((h, j, j - c0, len(chunks) - 1))
            last_item_of_chunk = {}
            for idx, it in enumerate(items):
                last_item_of_chunk[it[3]] = idx
            dma_state = {"next": 0}

            def issue_slab():
                ci = dma_state["next"]
                if ci >= len(chunks):
                    return
                dma_state["next"] += 1
                h, c0, c1, _ = chunks[ci]
                (ka, r_ka), (va, r_va) = ksl.next(), vsl.next()
                P.dma("sp", f"ld_k{(ksl.i - 1) % NSB}", lambda e: e.dma_start(
                    out=ka[0:64, 0:(c1 - c0) * 128], in_=kT_d[h, :, c0 * 128:c1 * 128]), writes=[r_ka], extra=[kw_tok])
                P.dma("sp", f"ld_v{(vsl.i - 1) % NSB}", lambda e: e.dma_start(
                    out=va[:, 0:c1 - c0, :], in_=v_d[:, h, c0:c1, :]), writes=[r_va], extra=[vw_tok])
                chunks[ci][3] = (ka, r_ka, va, r_va)

            for _ in range(NSB):
                issue_slab()
            return (n, Qa, r_Qa, nb, r_nb, items, chunks, last_item_of_chunk, issue_slab)

        def attention(T, gen, state):
            n, Qa, r_Qa, nb, r_nb, items, chunks, last_item_of_chunk, issue_slab = state
            attT, r_attT = attTs[T % 2]
            NI = len(items)
            sinfo = [None] * NI
            ob = {}

            def qk(idx):
                h, j, jl, ci = items[idx]
                ka, r_ka, va, r_va = chunks[ci][3]
                sbk, r_sb = Sb.next()
                qlo = max(0, j - 4 * T) * 128
                diag = j >= 4 * T
                fns = [lambda e: e.matmul(sbk[:, qlo:TQ], lhsT=ka[0:65, jl * 128:(jl + 1) * 128], rhs=Qa[0:65, h, qlo:TQ],
                                          start=True, stop=not diag)]
                if diag:
                    fns.append(lambda e: e.matmul(sbk[:, qlo:qlo + 128], lhsT=idn[:], rhs=mneg[:], start=False, stop=True))
                P.group("pe", fns, reads=[r_ka, r_Qa, r_idn, r_mneg], writes=[r_sb])
                pt, r_pt = Pt.next()
                P.op("act", lambda e: e.activation(out=pt[:, qlo:TQ], in_=sbk[:, qlo:TQ], func=AF.Exp,
                                                   bias=nb[:, j, h:h + 1], scale=1.0),
                     reads=[r_sb, r_nb], writes=[r_pt])
                sinfo[idx] = (pt, r_pt, qlo)

            def pv(idx):
                h, j, jl, ci = items[idx]
                ka, r_ka, va, r_va = chunks[ci][3]
                pt, r_pt, qlo = sinfo[idx]
                if j == 0:
                    ob[h] = Ob.next()
                o, r_o = ob[h]
                first, last = (j == 0), (j == n - 1)
                P.op("pe", lambda e: e.matmul(o[:, qlo:TQ], lhsT=va[:, jl, :], rhs=pt[:, qlo:TQ],
                                              start=first, stop=last), reads=[r_va, r_pt], writes=[r_o])
                if last:
                    rc, r_rc = gtmps[1]
                    P.op("dve", lambda e: e.reciprocal(out=rc[64:128, :], in_=o[64:128, :]), reads=[r_o], writes=[r_rc])
                    hf = (h % 2) * 64
                    P.op("dve", lambda e: e.tensor_tensor(out=attT[hf:hf + 64, h // 2, :], in0=o[0:64, :], in1=rc[64:128, :],
                                                          op=ALU.mult), reads=[r_o, r_rc], writes=[r_attT])

            gstate = {"gen": gen, "wait": 0}

            def tick():
                if gstate["gen"] is None:
                    return
                if gstate["wait"] > 0:
                    gstate["wait"] -= 1
                    return
                try:
                    gstate["wait"] = next(gstate["gen"]) or 0
                except StopIteration:
                    gstate["gen"] = None

            for idx in range(NI + 2):
                if idx < NI:
                    qk(idx)
                if idx >= 2:
                    pv(idx - 2)
                    if last_item_of_chunk[items[idx - 2][3]] == idx - 2:
                        issue_slab()
                tick()
            advance(gstate["gen"], 100000)

        def post(T):
            sgT, r_sgT = sgTs[T % 2]
            attT, r_attT = attTs[T % 2]
            for b in range(4):
                xo_, r_xo = xo.next()
                P.dma("sp", f"ld_xo{(xo.i - 1) % 2}", lambda e, b=b, xo_=xo_: e.dma_start(
                    out=xo_[:], in_=x_d[T * TQ + b * 128:T * TQ + (b + 1) * 128, :]), writes=[r_xo])
                yield 4
                pas = []
                for c in range(2):
                    pa, r_pa = Ab.next()
                    fns = [lambda e, hp=hp, b=b, c=c, pa=pa: e.matmul(
                        pa[:], lhsT=attT[:, hp, b * 128:(b + 1) * 128], rhs=woa[:, hp, c * 512:(c + 1) * 512],
                        start=(hp == 0), stop=False) for hp in range(4)]
                    fns += [lambda e, gp=gp, b=b, c=c, pa=pa: e.matmul(
                        pa[:], lhsT=sgT[:, gp, b * 128:(b + 1) * 128], rhs=wos[:, gp, c * 512:(c + 1) * 512],
                        start=False, stop=(gp == 3)) for gp in range(4)]
                    P.group("pe", fns, reads=[r_attT, r_sgT, r_woa, r_wos], writes=[r_pa])
                    pas.append((c, pa, r_pa))
                yield 5
                for c, pa, r_pa in pas:
                    P.op("dve", lambda e, c=c, pa=pa, xo_=xo_: e.tensor_tensor(
                        out=xo_[:, c * 512:(c + 1) * 512], in0=pa[:], in1=xo_[:, c * 512:(c + 1) * 512], op=ALU.add),
                        reads=[r_pa], writes=[r_xo])
                yield 2
                h1_tokens.append(P.dma("sp", f"st_h1{(xo.i - 1) % 2}", lambda e, b=b, xo_=xo_: e.dma_start(
                    out=out_d[T * TQ + b * 128:T * TQ + (b + 1) * 128, :], in_=xo_[:]), reads=[r_xo]))

        g0 = pre(0)
        advance(g0, 1000)
        r_cwtd = Res()
        for w in range(3):
            P.dma("sp", "c_cwd", lambda e, w=w: e.dma_start(out=cwt_d[:, :, w], in_=cw_d[w].rearrange("(c p) -> p c", p=128),
                                                             allow_slow_non_contiguous=True), writes=[r_cwtd])
        P.dma("sp", "c_cwd", lambda e: e.dma_start(out=cwt_d[:, :, 3], in_=cb_d.rearrange("(c p) -> p c", p=128),
                                                   allow_slow_non_contiguous=True), writes=[r_cwtd])
        r_wupd, r_wdnd = Res(), Res()
        for kc in range(KC):
            P.dma("pool", "w_upd", lambda e, kc=kc: e.dma_start(out=wupb_d[kc * 128:(kc + 1) * 128, :],
                                                                 in_=wup_d[kc * 128:(kc + 1) * 128, :],
                                                                 max_dma_last_dim=4096), writes=[r_wupd], extra=[kw_toks[0]])
        for i in range(0, NPAIR, 2):
            P.dma("pool", "w_dnd", lambda e, i=i: e.dma_start(out=wdnb_d[i * 128:(i + 2) * 128, :],
                                                               in_=wdn_d[i * 128:(i + 2) * 128, :],
                                                               max_dma_last_dim=4096), writes=[r_wdnd], extra=[kw_toks[0]])


        def chain(gens):
            for g_ in gens:
                for v_ in g_:
                    yield v_

        state = att_setup(0)
        pending_post = None
        for T in range(NT):
            gens = []
            if pending_post is not None:
                gens.append(pending_post)
            if T + 1 < NT:
                gens.append(pre(T + 1))
            attention(T, chain(gens) if gens else None, state)
            if T + 1 < NT:
                state = att_setup(T + 1)
            pending_post = post(T)
        advance(pending_post, 100000)

        wtoks = [r_wupd.w, r_wdnd.w, r_cwtd.w]
        P.emit(final_waits=h1_tokens[-4:] + wtoks, gate_inc=gate)

    if only_pass1:
        gate_cm.__exit__(None, None, None)
        return nc
    with ExitStack() as es:
        def sb(name, shape, dt):
            return es.enter_context(nc.sbuf_tensor(name, shape, dt))

        def ps(name, shape, dt):
            return es.enter_context(nc.psum_tensor(name, shape, dt))

        P = Prog(nc)
        wup = sb("wup", [128, KC, 2 * DFF], BF16); r_wupq = [Res() for _ in range(4)]
        wdn = sb("wdn", [128, NPAIR, D], BF16); r_wdn = Res()
        xt = sb("xt2", [128, 5, D], F32); r_xt = [Res() for _ in range(5)]
        xnb = Rot([sb(f"xnb2{i}", [128, D], BF16) for i in range(2)])
        xnT = sb("xnT2", [128, KC, TQ], BF16); r_xnT = Res()
        mT = sb("mT", [128, NPAIR, TQ], BF16); r_mT = [Res() for _ in range(NPAIR)]
        ya = Rot([sb(f"ya{i}", [128, TQ], F32) for i in range(3)])
        yg = Rot([sb(f"yg{i}", [128, TQ], F32) for i in range(3)])
        sgl = Rot([sb(f"sgl{i}", [128, TQ], F32) for i in range(1)])
        carry = sb("carry", [128, 2, 2 * NPAIR, 2], F32); r_cy = [[Res() for _ in range(2 * NPAIR)] for _ in range(2)]
        cwt = sb("cwt", [128, 2 * NPAIR, 4], F32); r_cwt = Res()
        g2b = sb("g2b", [128, KC], F32); r_g2b = Res()
        g3b = sb("g3b", [128, D], F32); r_g3b = Res()
        idn = sb("idn2", [128, 128], BF16); r_idn = Res()
        onesf = sb("onesf2", [128, 128], F32); r_onesf = Res()
        cst = sb("cst2", [128, 2], F32); r_cst = Res()
        stt = (sb("ssq2", [128, 4], F32), sb("lnv2", [128, 4], F32), sb("rstd2", [128, 4], F32), [Res() for _ in range(4)])
        st3 = (sb("ssq3", [128, 4], F32), sb("lnv3", [128, 4], F32), sb("rstd3", [128, 4], F32), [Res() for _ in range(4)])
        trb = ps("trb2", [128, 1024], BF16); r_trb = Res()
        pbs = [ps(f"pc{i}", [128, 512], F32) for i in range(7)]
        Ab = Rot(pbs[0:5])
        Db = Rot(pbs[5:7])

        P.op("pool", lambda e: e.memset(onesf[:], 1.0), writes=[r_onesf])
        P.op("pool", lambda e: e.memset(cst[:, 0:1], EPS), writes=[r_cst])
        P.op("pool", lambda e: e.memset(carry[:], 0.0), writes=r_cy[0] + r_cy[1])
        P.op("pool", lambda e: e.affine_select(out=idn[:], in_=onesf[:], pattern=[[-1, 128]], compare_op=ALU.is_equal,
                                               fill=0.0, base=0, channel_multiplier=1), reads=[r_onesf], writes=[r_idn])
        P.dma("sp", "c_g2", lambda e: e.dma_start(out=g2b[:], in_=g2_d.rearrange("(k p) -> p k", p=128), allow_slow_non_contiguous=True), writes=[r_g2b])
        P.dma("sp", "c_g3", lambda e: e.dma_start(out=g3b[:], in_=g3_d.partition_broadcast(128)), writes=[r_g3b])
        P.dma("sp", "c_cw", lambda e: e.dma_start(out=cwt[:], in_=cwt_d[:, :, :]), writes=[r_cwt])
        wup_src = wupb_d.rearrange("(k p) n -> p k n", p=128)
        for q in range(4):
            c0, c1 = q * 768, min((q + 1) * 768, DFF)
            for off in (0, DFF):
                P.dma("sp", f"w_up{q}", lambda e, c0=c0, c1=c1, off=off: e.dma_start(
                    out=wup[:, :, off + c0:off + c1], in_=wup_src[:, :, off + c0:off + c1]), writes=[r_wupq[q]])
        P.dma("sp", "w_dn", lambda e: e.dma_start(out=wdn[:], in_=wdnb_d.rearrange("(c p) n -> p c n", p=128)), writes=[r_wdn])

        out_toks = [None] * 4
        all_out = []
        ssq2, lnv2, rstd2, r_c2 = stt

        def norm_block(T, b):
            sl = (4 * T + b) % 5
            P.dma("sp", f"ld_h{sl}", lambda e: e.dma_start(
                out=xt[:, sl, :], in_=out_d[T * TQ + b * 128:T * TQ + (b + 1) * 128, :]), writes=[r_xt[sl]])
            rc_ = r_c2[b]
            xb, r_xb = xnb.next()
            P.op("act", lambda e: e.activation(out=xb[:], in_=xt[:, sl, :], func=AF.Square,
                                               accum_out=ssq2[:, b:b + 1]), reads=[r_xt[sl]], writes=[r_xb, rc_])
            P.op("act", lambda e: e.activation(out=lnv2[:, b:b + 1], in_=ssq2[:, b:b + 1], func=AF.Ln, bias=cst[:, 0:1],
                                               scale=1.0 / D), reads=[r_cst], writes=[rc_])
            P.op("act", lambda e: e.activation(out=rstd2[:, b:b + 1], in_=lnv2[:, b:b + 1], func=AF.Exp, scale=-0.5),
                 writes=[rc_])
            P.op("dve", lambda e: e.tensor_scalar(out=xb[:], in0=xt[:, sl, :], scalar1=rstd2[:, b:b + 1],
                                                  scalar2=None, op0=ALU.mult), reads=[r_xt[sl], rc_], writes=[r_xb])
            return xb, r_xb

        def transp_block(b, xb, r_xb):
            P.group("pe", [lambda e, kc=kc: e.transpose(trb[:, kc * 128:(kc + 1) * 128],
                                                        xb[:, kc * 128:(kc + 1) * 128], idn[:])
                           for kc in range(KC)], reads=[r_xb, r_idn], writes=[r_trb])
            P.op("dve", lambda e: e.tensor_tensor(out=xnT[:, :, b * 128:(b + 1) * 128],
                                                  in0=trb[:].rearrange("p (k t) -> p k t", t=128),
                                                  in1=g2b[:, :].unsqueeze(2).broadcast_to([128, KC, 128]), op=ALU.mult),
                 reads=[r_trb, r_g2b], writes=[r_xnT])

        for b in range(4):
            xb_, r_xb_ = norm_block(0, b)
            transp_block(b, xb_, r_xb_)
        for T in range(NT):
            pend = {}
            cur, nxt = T % 2, (T + 1) % 2
            prev_info = None
            prev2_info = None

            def finish(pinfo):
                ip, infos = pinfo
                (a_y, r_a), (g_y, r_g) = [(x[3], x[4]) for x in infos]
                sg, r_sg = sgl.next()
                P.op("act", lambda e: e.activation(out=sg[:], in_=g_y[:], func=AF.Silu), reads=[r_g], writes=[r_sg])
                P.op("pool", lambda e: e.tensor_tensor(out=mT[:, ip, :], in0=sg[:], in1=a_y[:], op=ALU.mult),
                     reads=[r_sg, r_a], writes=[r_mT[ip]])

            def fix1(x, cur=cur):
                ci, pa, r_pa, y, r_y = x
                P.op("dve", lambda e: e.scalar_tensor_tensor(
                    out=y[:, 0:2], in0=carry[:, cur, ci, 0:2], scalar=cwt[:, ci, 0:1], in1=y[:, 0:2], op0=ALU.mult, op1=ALU.add),
                    reads=[r_cy[cur][ci], r_cwt], writes=[r_y])

            def fix2(x, cur=cur):
                ci, pa, r_pa, y, r_y = x
                P.op("dve", lambda e: e.scalar_tensor_tensor(
                    out=y[:, 0:1], in0=carry[:, cur, ci, 1:2], scalar=cwt[:, ci, 1:2], in1=y[:, 0:1], op0=ALU.mult, op1=ALU.add),
                    reads=[r_cy[cur][ci], r_cwt], writes=[r_y])

            for i in range(NPAIR):
                info = []
                for typ in range(2):
                    ci = typ * NPAIR + i
                    col = ci * 128
                    pa, r_pa = Ab.next()
                    P.group("pe", [lambda e, kc=kc, col=col, pa=pa: e.matmul(
                        pa[:], lhsT=wup[:, kc, col:col + 128], rhs=xnT[:, kc, :],
                        start=(kc == 0), stop=(kc == KC - 1)) for kc in range(KC)], reads=[r_wupq[i // 6], r_xnT], writes=[r_pa])
                    y, r_y = (ya if typ == 0 else yg).next()
                    P.op("act", lambda e, ci=ci, pa=pa, y=y: e.activation(
                        out=y[:], in_=pa[:], func=AF.Identity, scale=cwt[:, ci, 2:3], bias=cwt[:, ci, 3:4]),
                        reads=[r_pa, r_cwt], writes=[r_y])
                    info.append((ci, pa, r_pa, y, r_y))
                if prev2_info is not None:
                    finish(prev2_info)
                    prev2_info = None
                pv_ = prev_info[1] if prev_info is not None else None
                for k_, (ci, pa, r_pa, y, r_y) in enumerate(info):
                    P.op("dve", lambda e, ci=ci, pa=pa, y=y: e.scalar_tensor_tensor(
                        out=y[:, 1:TQ], in0=pa[:, 0:TQ - 1], scalar=cwt[:, ci, 1:2], in1=y[:, 1:TQ], op0=ALU.mult, op1=ALU.add),
                        reads=[r_pa, r_cwt], writes=[r_y])
                    if pv_ is not None:
                        fix1(pv_[k_])
                for k_, (ci, pa, r_pa, y, r_y) in enumerate(info):
                    P.op("dve", lambda e, ci=ci, pa=pa, y=y: e.scalar_tensor_tensor(
                        out=y[:, 2:TQ], in0=pa[:, 0:TQ - 2], scalar=cwt[:, ci, 0:1], in1=y[:, 2:TQ], op0=ALU.mult, op1=ALU.add),
                        reads=[r_pa, r_cwt], writes=[r_y])
                    if pv_ is not None:
                        fix2(pv_[k_])
                for (ci, pa, r_pa, y, r_y) in info:
                    P.op("dve", lambda e, ci=ci, pa=pa, nxt=nxt: e.tensor_copy(out=carry[:, nxt, ci, :], in_=pa[:, TQ - 2:TQ]),
                         reads=[r_pa], writes=[r_cy[nxt][ci]])
                prev2_info = prev_info
                prev_info = (i, info)
            if prev2_info is not None:
                finish(prev2_info)
            for x in prev_info[1]:
                fix1(x)
            for x in prev_info[1]:
                fix2(x)
            finish(prev_info)
            if T + 1 < NT:
                pend[0] = norm_block(T + 1, 0)
            jk, r_jk = yg.items[0]
            jkb = jk[:].bitcast(BF16)
            for b in range(4):
                sl = (4 * T + b) % 5
                dbanks = [Db.next(), Db.next()]
                for c in range(2):
                    pa, r_pa = dbanks[c]
                    P.group("pe", [lambda e, i=i, b=b, c=c, pa=pa: e.matmul(
                        pa[:], lhsT=mT[:, i, b * 128:(b + 1) * 128], rhs=wdn[:, i, c * 512:(c + 1) * 512],
                        start=(i == 0), stop=False) for i in range(16)], reads=r_mT[0:16] + [r_wdn], writes=[r_pa])
                for c in range(2):
                    pa, r_pa = dbanks[c]
                    P.group("pe", [lambda e, i=i, b=b, c=c, pa=pa: e.matmul(
                        pa[:], lhsT=mT[:, i, b * 128:(b + 1) * 128], rhs=wdn[:, i, c * 512:(c + 1) * 512],
                        start=False, stop=(i == NPAIR - 1)) for i in range(16, NPAIR)], reads=r_mT[16:] + [r_wdn], writes=[r_pa])
                    P.op("dve", lambda e, sl=sl, c=c, pa=pa: e.tensor_tensor(
                        out=xt[:, sl, c * 512:(c + 1) * 512], in0=pa[:], in1=xt[:, sl, c * 512:(c + 1) * 512], op=ALU.add),
                        reads=[r_pa], writes=[r_xt[sl]])
                ssq, lnv, rstd, r_c3 = st3
                rc_ = r_c3[b]
                P.op("act", lambda e, b=b, sl=sl: e.activation(out=jkb, in_=xt[:, sl, :], func=AF.Square,
                                                              accum_out=ssq[:, b:b + 1]), reads=[r_xt[sl]], writes=[r_jk, rc_])
                P.op("act", lambda e, b=b: e.activation(out=lnv[:, b:b + 1], in_=ssq[:, b:b + 1], func=AF.Ln, bias=cst[:, 0:1],
                                                        scale=1.0 / D), reads=[r_cst], writes=[rc_])
                P.op("act", lambda e, b=b: e.activation(out=rstd[:, b:b + 1], in_=lnv[:, b:b + 1], func=AF.Exp, scale=-0.5),
                     writes=[rc_])
                P.op("dve", lambda e, b=b, sl=sl: e.scalar_tensor_tensor(out=xt[:, sl, :], in0=xt[:, sl, :], scalar=rstd[:, b:b + 1],
                                                                        in1=g3b[:], op0=ALU.mult, op1=ALU.mult),
                     reads=[rc_, r_g3b], writes=[r_xt[sl]])
                out_toks[b] = P.dma("sp", f"st_o{sl}", lambda e, T=T, b=b, sl=sl: e.dma_start(
                    out=out_d[T * TQ + b * 128:T * TQ + (b + 1) * 128, :], in_=xt[:, sl, :]), reads=[r_xt[sl]])
                all_out.append(out_toks[b])
                if T + 1 < NT:
                    if b + 1 < 4:
                        pend[b + 1] = norm_block(T + 1, b + 1)
                    transp_block(b, *pend[b])
        P.emit(final_waits=all_out[-8:], gate_wait=(gate, 1))
    gate_cm.__exit__(None, None, None)
    return nc


_NC_CACHE = {}


def _get_nc(S):
    if S not in _NC_CACHE:
        _NC_CACHE[S] = build_nc(S)
    return _NC_CACHE[S]


def make_in_maps(inputs, B, S):
    f = lambda a: np.ascontiguousarray(np.asarray(a, dtype=np.float32))
    shared = {
        "norm_mix_g": f(inputs["norm_mix_g"]).reshape(D),
        "w_in": f(inputs["w_in"]).reshape(D, IN_COLS),
        "b_forget": f(inputs["b_forget"]).reshape(H),
        "gmlp_norm_g": f(inputs["gmlp_norm_g"]).reshape(512),
        "w_spatial": f(inputs["w_spatial"]).reshape(8, 128, 128),
        "b_spatial": f(inputs["b_spatial"]).reshape(8, 128),
        "w_out": f(inputs["w_out"]).reshape(D, D),
        "norm_ffn_g": f(inputs["norm_ffn_g"]).reshape(D),
        "w_up": f(inputs["w_up"]).reshape(D, 2 * DFF),
        "conv_w": f(inputs["conv_w"]).reshape(3, 2 * DFF),
        "conv_b": f(inputs["conv_b"]).reshape(2 * DFF),
        "w_down": f(inputs["w_down"]).reshape(DFF, D),
        "norm_final_g": f(inputs["norm_final_g"]).reshape(D),
    }
    x = f(inputs["x"])
    return [dict(shared, x=np.ascontiguousarray(x[b])) for b in range(B)]


def kernel(**inputs):
    x = np.asarray(inputs["x"])
    B, S, _ = x.shape
    nc = _get_nc(S)
    in_maps = make_in_maps(inputs, B, S)
    res = run_bass_kernel_spmd(nc, in_maps, core_ids=list(range(B)))
    return np.stack([np.asarray(r["out"]) for r in res.results], axis=0).astype(np.float32)
```

```python
import numpy as np
from contextlib import ExitStack
import concourse.bass as bass
import concourse.mybir as mybir
from concourse.bass_utils import run_bass_kernel_spmd

F32 = mybir.dt.float32
BF16 = mybir.dt.bfloat16
AF = mybir.ActivationFunctionType
ALU = mybir.AluOpType
AX = mybir.AxisListType

D = 1024
KC = 8
H = 8
DFF = 2816
NPAIR = DFF // 128
IN_COLS = 2568
EPS = 1e-6
TQ = 512
CH = 16
NSB = 3
ENG_NAMES = ["pe", "act", "dve", "pool", "sp"]


class Res:
    def __init__(self):
        self.w = None
        self.r = []


class Prog:
    def __init__(self, nc):
        self.nc = nc
        self.ops = {e: [] for e in ENG_NAMES}
        self.cnt = {}
        self.sem_keys = []

    def _bump(self, key, n):
        if key not in self.cnt:
            self.cnt[key] = 0
            self.sem_keys.append(key)
        self.cnt[key] += n
        return (key, self.cnt[key])

    def _deps(self, reads, writes, extra):
        w = [t for t in extra if t is not None]
        for r in reads:
            if r.w is not None:
                w.append(r.w)
        for r in writes:
            if r.w is not None:
                w.append(r.w)
            w.extend(r.r)
        return w

    def _done(self, tok, reads, writes):
        for r in reads:
            r.r.append(tok)
        for r in writes:
            r.w = tok
            r.r = []

    def op(self, eng, fn, reads=(), writes=(), extra=()):
        waits = self._deps(reads, writes, extra)
        tok = self._bump(eng, 1)
        self.ops[eng].append((fn, waits, (eng, 1)))
        self._done(tok, reads, writes)
        return tok

    def group(self, eng, fns, reads=(), writes=(), extra=()):
        waits = self._deps(reads, writes, extra)
        for i, fn in enumerate(fns):
            last = i == len(fns) - 1
            if last:
                tok = self._bump(eng, 1)
            self.ops[eng].append((fn, waits if i == 0 else [], (eng, 1) if last else None))
        self._done(tok, reads, writes)
        return tok

    def dma(self, eng, semkey, fn, reads=(), writes=(), extra=()):
        waits = self._deps(reads, writes, extra)
        tok = self._bump(semkey, 16)
        self.ops[eng].append((fn, waits, (semkey, 16)))
        self._done(tok, reads, writes)
        return tok

    def emit(self, final_waits=(), gate_wait=None, gate_inc=None):
        nc = self.nc
        with ExitStack() as es:
            sems = {}
            for k in self.sem_keys:
                sems[k] = es.enter_context(nc.semaphore("s_" + k))
            block = es.enter_context(nc.Block())
            engs = {"pe": block.tensor, "act": block.scalar, "dve": block.vector,
                    "pool": block.gpsimd, "sp": block.sync}
            for en in ENG_NAMES:
                ops = self.ops[en]
                fw = [t for t in final_waits if t is not None] if en == "sp" else []
                if not ops and not fw:
                    continue

                def body(e, ops=ops, fw=fw, en=en):
                    seen = {}
                    if gate_wait is not None:
                        e.wait_ge(gate_wait[0], gate_wait[1])
                    for fn, waits, inc in ops:
                        for (k, v) in waits:
                            if seen.get(k, 0) < v:
                                e.wait_ge(sems[k], v)
                                seen[k] = v
                        ins = fn(e)
                        if inc is not None:
                            ins.then_inc(sems[inc[0]], inc[1])
                    for (k, v) in fw:
                        if seen.get(k, 0) < v:
                            e.wait_ge(sems[k], v)
                            seen[k] = v
                    if gate_inc is not None and en == "sp":
                        e.sem_inc(gate_inc, 1)
                engs[en](body)


class Rot:
    def __init__(self, items):
        self.items = [(a, Res()) for a in items]
        self.i = 0

    def next(self):
        it = self.items[self.i % len(self.items)]
        self.i += 1
        return it


def _norm_transpose(P, xt, r_xt, gbc, r_gbc, xnb, xnT, r_xnT, trb, r_trb, idn, r_idn, cst, r_cst, junk, r_junk, st):
    ssq, lnv, rstd, r_cols = st
    for b in range(4):
        rc_ = r_cols[b]
        P.op("act", lambda e, b=b: e.activation(out=junk[:], in_=xt[:, b, :], func=AF.Square,
                                                accum_out=ssq[:, b:b + 1]),
             reads=[r_xt[b]], writes=[r_junk, rc_])
        P.op("act", lambda e, b=b: e.activation(out=lnv[:, b:b + 1], in_=ssq[:, b:b + 1], func=AF.Ln, bias=cst[:, 0:1],
                                                scale=1.0 / D), reads=[r_cst], writes=[rc_])
        P.op("act", lambda e, b=b: e.activation(out=rstd[:, b:b + 1], in_=lnv[:, b:b + 1], func=AF.Exp, scale=-0.5),
             writes=[rc_])
    for b in range(4):
        rc_ = r_cols[b]
        xb, r_xb = xnb.next()
        P.op("dve", lambda e, b=b, xb=xb: e.tensor_scalar(out=xb[:], in0=xt[:, b, :], scalar1=rstd[:, b:b + 1],
                                                          scalar2=None, op0=ALU.mult),
             reads=[r_xt[b], rc_], writes=[r_xb])
        P.group("pe", [lambda e, kc=kc, xb=xb: e.transpose(trb[:, kc * 128:(kc + 1) * 128],
                                                            xb[:, kc * 128:(kc + 1) * 128], idn[:])
                       for kc in range(KC)], reads=[r_xb, r_idn], writes=[r_trb])
        P.op("dve", lambda e, b=b: e.tensor_tensor(out=xnT[:, :, b * 128:(b + 1) * 128],
                                                   in0=trb[:].rearrange("p (k t) -> p k t", t=128),
                                                   in1=gbc[:, :].unsqueeze(2).broadcast_to([128, KC, 128]), op=ALU.mult),
             reads=[r_trb, r_gbc], writes=[r_xnT])


def build_nc(S=8192, only_pass1=False):
    assert S % TQ == 0
    NT = S // TQ
    NBLK = S // 128
    nc = bass.Bass("TRN2", target_bir_lowering=False)

    def din(name, shape):
        return nc.dram_tensor(name, shape, F32, kind="ExternalInput").ap()

    x_d = din("x", [S, D])
    g1_d = din("norm_mix_g", [D])
    win_d = din("w_in", [D, IN_COLS])
    bf_d = din("b_forget", [H])
    gg_d = din("gmlp_norm_g", [512])
    wsp_d = din("w_spatial", [8, 128, 128])
    bsp_d = din("b_spatial", [8, 128])
    wout_d = din("w_out", [D, D])
    g2_d = din("norm_ffn_g", [D])
    wup_d = din("w_up", [D, 2 * DFF])
    cw_d = din("conv_w", [3, 2 * DFF])
    cb_d = din("conv_b", [2 * DFF])
    wdn_d = din("w_down", [DFF, D])
    g3_d = din("norm_final_g", [D])
    out_d = nc.dram_tensor("out", [S, D], F32, kind="ExternalOutput").ap()
    kT_d = nc.dram_tensor("kT_scr", [H, 64, S], BF16, kind="Internal").ap()
    v_d = nc.dram_tensor("v_scr", [128, H, NBLK, 128], BF16, kind="Internal").ap()
    wupb_d = nc.dram_tensor("wup_scr", [D, 2 * DFF], BF16, kind="Internal").ap()
    wdnb_d = nc.dram_tensor("wdn_scr", [DFF, D], BF16, kind="Internal").ap()

    h1_tokens = []
    gate_cm = nc.semaphore("gate")
    gate = gate_cm.__enter__()

    with ExitStack() as es:
        def sb(name, shape, dt):
            return es.enter_context(nc.sbuf_tensor(name, shape, dt))

        def ps(name, shape, dt):
            return es.enter_context(nc.psum_tensor(name, shape, dt))

        P = Prog(nc)
        win = sb("win", [128, KC, IN_COLS], BF16); r_win = Res()
        woa = sb("woa", [128, 4, D], BF16); r_woa = Res()
        wos = sb("wos", [128, 4, D], BF16); r_wos = Res()
        xs = Rot([sb(f"xs{i}", [128, D], F32) for i in range(2)])
        xo = Rot([sb(f"xo{i}", [128, D], F32) for i in range(2)])
        xnb = Rot([sb(f"xnb{i}", [128, D], BF16) for i in range(2)])
        xnT = sb("xnT", [128, KC, TQ], BF16); r_xnT = Res()
        Qas = [(sb(f"Qa{i}", [65, H, TQ], BF16), Res()) for i in range(2)]
        kst = sb("kst", [64, H, TQ], BF16); r_kst = Res()
        vst = sb("vst", [128, H, 4, 128], BF16); r_vst = Res()
        ksl = Rot([sb(f"ksl{i}", [65, CH * 128], BF16) for i in range(NSB)])
        vsl = Rot([sb(f"vsl{i}", [128, CH, 128], BF16) for i in range(NSB)])
        Pt = Rot([sb(f"Pt{i}", [128, TQ], BF16) for i in range(3)])
        attTs = [(sb(f"attT{i}", [128, 4, TQ], BF16), Res()) for i in range(2)]
        uT = sb("uT", [128, 4, TQ], BF16); r_uT = Res()
        sgTs = [(sb(f"sgT{i}", [128, 4, TQ], BF16), Res()) for i in range(2)]
        vgf = Rot([sb(f"vgf{i}", [128, 512], F32) for i in range(2)])
        sqs = [(sb(f"sq{i}", [128, 512], F32), Res()) for i in range(2)]
        vgn = sb("vgn", [128, 4, 512], BF16); r_vgn = Res()
        gtmps = [(sb(f"gtmp{i}", [128, TQ], F32), Res()) for i in range(2)]
        gbc = sb("gbc", [128, KC], F32); r_gbc = Res()
        gnb = sb("gnb", [128, 512], F32); r_gnb = Res()
        idn = sb("idn", [128, 128], BF16); r_idn = Res()
        onesf = sb("onesf", [128, 128], F32); r_onesf = Res()
        negf = sb("negf", [128, 128], F32); r_negf = Res()
        utri = sb("utri", [128, 128], F32); r_utri = Res()
        mneg = sb("mneg", [128, 128], BF16); r_mneg = Res()
        wspf = sb("wspf", [128, 128], F32); r_wspf = Res()
        wspb = sb("wspb", [128, 128], BF16); r_wspb = Res()
        wspT = sb("wspT", [128, 8, 128], BF16); r_wspT = Res()
        bS = sb("bS", [128, 4, 128], F32); r_bS = Res()
        bfb = sb("bfb", [128, H], F32); r_bfb = Res()
        cst = sb("cst", [128, 2], F32); r_cst = Res()
        stt = (sb("ssq", [128, 4], F32), sb("lnv", [128, 4], F32), sb("rstd", [128, 4], F32), Res())
        gsss = [(sb(f"gss{i}", [128, 8], F32), Res()) for i in range(2)]
        zf = sb("zf", [128, 4, H], F32); r_zf = Res()
        nl = sb("nl", [128, 4, H], F32); r_nl = Res()
        Aall = sb("Aall", [128, NBLK, H], F32); r_Aall = Res()
        carryB = sb("carryB", [128, NBLK + 1, H], F32); r_carry = Res()
        nbs = [(sb(f"nb{i}", [128, NBLK, H], F32), Res()) for i in range(1)] * 2

        trb = ps("trb", [128, 1024], BF16); r_trb = Res()
        pbs = [ps(f"pb{i}", [128, 512], F32) for i in range(7)]
        Ab = Rot(pbs[0:2])
        Sb = Rot(pbs[2:5])
        Ob = Rot(pbs[5:7])

        P.op("pool", lambda e: e.memset(onesf[:], 1.0), writes=[r_onesf])
        P.op("pool", lambda e: e.memset(negf[:], -30000.0), writes=[r_negf])
        P.op("pool", lambda e: e.memset(cst[:, 0:1], EPS), writes=[r_cst])
        P.op("pool", lambda e: e.memset(cst[:, 1:2], 1.0), writes=[r_cst])
        P.op("pool", lambda e: e.affine_select(out=idn[:], in_=onesf[:], pattern=[[-1, 128]], compare_op=ALU.is_equal,
                                               fill=0.0, base=0, channel_multiplier=1), reads=[r_onesf], writes=[r_idn])
        P.op("pool", lambda e: e.affine_select(out=utri[:], in_=onesf[:], pattern=[[1, 128]], compare_op=ALU.is_ge,
                                               fill=0.0, base=0, channel_multiplier=-1), reads=[r_onesf], writes=[r_utri])
        P.op("pool", lambda e: e.affine_select(out=mneg[:], in_=negf[:], pattern=[[-1, 128]], compare_op=ALU.is_gt,
                                               fill=0.0, base=0, channel_multiplier=1), reads=[r_negf], writes=[r_mneg])
        P.op("pool", lambda e: e.memset(carryB[:, 0, :], 0.0), writes=[r_carry])
        P.op("pool", lambda e: e.memset(vst[:, :, :, 64:128], 1.0), writes=[r_vst])
        for (a, r) in ksl.items:
            P.op("pool", lambda e, a=a: e.memset(a[64:65, :], 1.0), writes=[r])
        P.dma("sp", "c_gbc", lambda e: e.dma_start(out=gbc[:], in_=g1_d.rearrange("(k p) -> p k", p=128), allow_slow_non_contiguous=True), writes=[r_gbc])
        P.dma("sp", "c_gnb", lambda e: e.dma_start(out=gnb[:], in_=gg_d.partition_broadcast(128)), writes=[r_gnb])
        P.dma("sp", "c_bfb", lambda e: e.dma_start(out=bfb[:], in_=bf_d.partition_broadcast(128)), writes=[r_bfb])
        for g in range(8):
            gp, gg2 = g // 2, g % 2
            P.dma("sp", "c_bS", lambda e, g=g, gp=gp, gg2=gg2: e.dma_start(
                out=bS[gg2 * 64:(gg2 + 1) * 64, gp, :], in_=bsp_d[g].partition_broadcast(64)), writes=[r_bS])
        win_src = win_d.rearrange("(k p) n -> p k n", p=128)
        r_winq = {g: Res() for g in "KVQFUG"}
        for g, (c0, c1) in (("K", (512, 1024)), ("V", (1024, 1536)), ("Q", (0, 512)), ("F", (2560, 2568)),
                            ("U", (1536, 2048)), ("G", (2048, 2560))):
            P.dma("pool", "w_in" + g, lambda e, c0=c0, c1=c1: e.dma_start(out=win[:, :, c0:c1], in_=win_src[:, :, c0:c1],
                                                                        max_dma_last_dim=4096), writes=[r_winq[g]])
        P.dma("pool", "w_oa", lambda e: e.dma_start(out=woa[:], in_=wout_d[0:512, :].rearrange("(c p) n -> p c n", p=128),
                                                    max_dma_last_dim=4096), writes=[r_woa])
        P.dma("pool", "w_os", lambda e: e.dma_start(out=wos[:], in_=wout_d[512:1024, :].rearrange("(c p) n -> p c n", p=128),
                                                    max_dma_last_dim=4096), writes=[r_wos])
        for g in range(8):
            P.dma("sp", "c_wsp", lambda e, g=g: e.dma_start(out=wspf[:], in_=wsp_d[g]), writes=[r_wspf])
            P.op("pool", lambda e: e.affine_select(out=wspb[:], in_=wspf[:], pattern=[[-1, 128]], compare_op=ALU.is_ge,
                                                   fill=0.0, base=0, channel_multiplier=1), reads=[r_wspf], writes=[r_wspb])
            P.op("pe", lambda e: e.transpose(trb[:, 0:128], wspb[:], idn[:]), reads=[r_wspb, r_idn], writes=[r_trb])
            P.op("dve", lambda e, g=g: e.tensor_copy(out=wspT[:, g, :], in_=trb[:, 0:128]), reads=[r_trb], writes=[r_wspT])
        kw_toks, vw_toks = {}, {}
        r_cols = [Res() for _ in range(4)]
        ssq, lnv, rstd, _ = stt

        def pre(T):
            n = 4 * T + 4
            Qa, r_Qa = Qas[T % 2]
            nb, r_nb = nbs[T % 2]
            sgT, r_sgT = sgTs[T % 2]
            for b in range(4):
                xs_, r_xs = xs.next()
                P.dma("sp", f"ld_x{(xs.i - 1) % 2}", lambda e, b=b, xs_=xs_: e.dma_start(
                    out=xs_[:], in_=x_d[T * TQ + b * 128:T * TQ + (b + 1) * 128, :]), writes=[r_xs])
                yield 6
                rc_ = r_cols[b]
                xb, r_xb = xnb.next()
                P.op("act", lambda e, b=b, xs_=xs_, xb=xb: e.activation(out=xb[:], in_=xs_[:], func=AF.Square,
                                                                        accum_out=ssq[:, b:b + 1]), reads=[r_xs], writes=[r_xb, rc_])
                yield 2
                P.op("act", lambda e, b=b: e.activation(out=lnv[:, b:b + 1], in_=ssq[:, b:b + 1], func=AF.Ln,
                                                        bias=cst[:, 0:1], scale=1.0 / D), reads=[r_cst], writes=[rc_])
                yield 1
                P.op("act", lambda e, b=b: e.activation(out=rstd[:, b:b + 1], in_=lnv[:, b:b + 1], func=AF.Exp, scale=-0.5),
                     writes=[rc_])
                yield 2
                P.op("dve", lambda e, b=b, xb=xb, xs_=xs_: e.tensor_scalar(out=xb[:], in0=xs_[:], scalar1=rstd[:, b:b + 1],
                                                                           scalar2=None, op0=ALU.mult),
                     reads=[r_xs, rc_], writes=[r_xb])
                yield 3
                P.group("pe", [lambda e, kc=kc, xb=xb: e.transpose(trb[:, kc * 128:(kc + 1) * 128],
                                                                    xb[:, kc * 128:(kc + 1) * 128], idn[:])
                               for kc in range(KC)], reads=[r_xb, r_idn], writes=[r_trb])
                yield 3
                P.op("dve", lambda e, b=b: e.tensor_tensor(out=xnT[:, :, b * 128:(b + 1) * 128],
                                                           in0=trb[:].rearrange("p (k t) -> p k t", t=128),
                                                           in1=gbc[:, :].unsqueeze(2).broadcast_to([128, KC, 128]), op=ALU.mult),
                     reads=[r_trb, r_gbc], writes=[r_xnT])
                yield 2
            for hp0 in (0, 2):
                pas = []
                for hp in (hp0, hp0 + 1):
                    pa, r_pa = Ab.next()
                    P.group("pe", [lambda e, kc=kc, hp=hp, pa=pa: e.matmul(
                        pa[:], lhsT=win[:, kc, 512 + hp * 128:512 + (hp + 1) * 128], rhs=xnT[:, kc, :],
                        start=(kc == 0), stop=(kc == KC - 1)) for kc in range(KC)], reads=[r_winq["K"], r_xnT], writes=[r_pa])
                    pas.append((hp, pa, r_pa))
                yield 5
                for hp, pa, r_pa in pas:
                    P.op("dve", lambda e, hp=hp, pa=pa: e.tensor_copy(out=kst[0:64, 2 * hp, :], in_=pa[0:64, :]),
                         reads=[r_pa], writes=[r_kst])
                    P.op("dve", lambda e, hp=hp, pa=pa: e.tensor_copy(out=kst[0:64, 2 * hp + 1, :], in_=pa[64:128, :]),
                         reads=[r_pa], writes=[r_kst])
                yield 2
            kw_toks[T] = P.dma("sp", "st_k", lambda e: e.dma_start(
                out=kT_d[:, :, T * TQ:(T + 1) * TQ].rearrange("h d t -> d h t"), in_=kst[:]), reads=[r_kst])
            for b0 in (0, 2):
                pas = []
                for b in (b0, b0 + 1):
                    pa, r_pa = Ab.next()
                    P.group("pe", [lambda e, kc=kc, b=b, pa=pa: e.matmul(
                        pa[:], lhsT=xnT[:, kc, b * 128:(b + 1) * 128], rhs=win[:, kc, 1024:1536],
                        start=(kc == 0), stop=(kc == KC - 1)) for kc in range(KC)], reads=[r_winq["V"], r_xnT], writes=[r_pa])
                    pas.append((b, pa, r_pa))
                yield 5
                for b, pa, r_pa in pas:
                    P.op("dve", lambda e, b=b, pa=pa: e.tensor_copy(
                        out=vst[:, :, b, 0:64], in_=pa[:].rearrange("p (h d) -> p h d", d=64)), reads=[r_pa], writes=[r_vst])
                yield 2
            vw_toks[T] = P.dma("sp", "st_v", lambda e: e.dma_start(out=v_d[:, :, 4 * T:4 * T + 4, :], in_=vst[:]),
                               reads=[r_vst])
            for hp0 in (0, 2):
                pas = []
                for hp in (hp0, hp0 + 1):
                    pa, r_pa = Ab.next()
                    P.group("pe", [lambda e, kc=kc, hp=hp, pa=pa: e.matmul(
                        pa[:], lhsT=win[:, kc, hp * 128:(hp + 1) * 128], rhs=xnT[:, kc, :],
                        start=(kc == 0), stop=(kc == KC - 1)) for kc in range(KC)], reads=[r_winq["Q"], r_xnT], writes=[r_pa])
                    pas.append((hp, pa, r_pa))
                yield 5
                for hp, pa, r_pa in pas:
                    P.op("dve", lambda e, hp=hp, pa=pa: e.tensor_scalar(out=Qa[0:64, 2 * hp, :], in0=pa[0:64, :],
                                                                        scalar1=0.125, scalar2=None, op0=ALU.mult),
                         reads=[r_pa], writes=[r_Qa])
                    P.op("dve", lambda e, hp=hp, pa=pa: e.tensor_scalar(out=Qa[0:64, 2 * hp + 1, :], in0=pa[64:128, :],
                                                                        scalar1=0.125, scalar2=None, op0=ALU.mult),
                         reads=[r_pa], writes=[r_Qa])
                yield 2
            pa, r_pa = Ab.next()
            for b in range(4):
                P.group("pe", [lambda e, kc=kc, b=b, pa=pa: e.matmul(
                    pa[:, b * 8:(b + 1) * 8], lhsT=xnT[:, kc, b * 128:(b + 1) * 128], rhs=win[:, kc, 2560:2568],
                    start=(kc == 0), stop=(kc == KC - 1)) for kc in range(KC)], reads=[r_winq["F"], r_xnT], writes=[r_pa])
            yield 4
            P.op("dve", lambda e, pa=pa: e.tensor_tensor(
                out=zf[:], in0=pa[:, 0:32].rearrange("p (b h) -> p b h", h=H),
                in1=bfb[:].unsqueeze(1).broadcast_to([128, 4, H]), op=ALU.add), reads=[r_pa, r_bfb], writes=[r_zf])
            yield 3
            P.op("act", lambda e: e.activation(out=zf[:], in_=zf[:], func=AF.Exp, scale=-1.0), writes=[r_zf])
            yield 2
            P.op("act", lambda e: e.activation(out=nl[:], in_=zf[:], func=AF.Ln, bias=cst[:, 1:2], scale=1.0),
                 reads=[r_zf, r_cst], writes=[r_nl])
            yield 2
            pa, r_pa = Ab.next()
            for b in range(4):
                P.op("pe", lambda e, b=b, pa=pa: e.matmul(pa[:, b * 8:(b + 1) * 8], lhsT=utri[:], rhs=nl[:, b, :],
                                                          start=True, stop=True), reads=[r_utri, r_nl], writes=[r_pa])
                P.op("pe", lambda e, b=b, pa=pa: e.matmul(pa[:, 32 + b * 8:32 + (b + 1) * 8], lhsT=onesf[:], rhs=nl[:, b, :],
                                                          start=True, stop=True), reads=[r_onesf, r_nl], writes=[r_pa])
            yield 4
            for b in range(4):
                blk = 4 * T + b
                P.op("dve", lambda e, b=b, blk=blk, pa=pa: e.tensor_tensor(
                    out=Aall[:, blk, :], in0=pa[:, b * 8:(b + 1) * 8], in1=carryB[:, blk, :], op=ALU.add),
                    reads=[r_pa, r_carry], writes=[r_Aall])
                P.op("dve", lambda e, b=b, blk=blk, pa=pa: e.tensor_tensor(
                    out=carryB[:, blk + 1, :], in0=pa[:, 32 + b * 8:32 + (b + 1) * 8], in1=carryB[:, blk, :], op=ALU.add),
                    reads=[r_pa], writes=[r_carry])
                yield 1
            P.op("dve", lambda e: e.tensor_tensor(
                out=Qa[64:65, :, :].rearrange("p h (b t) -> p h b t", t=128),
                in0=carryB[64:65, 4 * T, :].unsqueeze(2).unsqueeze(3).broadcast_to([1, H, 4, 128]),
                in1=carryB[64:65, 4 * T:4 * T + 4, :].rearrange("p b h -> p h b").unsqueeze(3).broadcast_to([1, H, 4, 128]),
                op=ALU.subtract), reads=[r_carry], writes=[r_Qa])
            yield 1
            for gp0 in (0, 2):
                pas = []
                for gp in (gp0, gp0 + 1):
                    pa, r_pa = Ab.next()
                    P.group("pe", [lambda e, kc=kc, gp=gp, pa=pa: e.matmul(
                        pa[:], lhsT=win[:, kc, 1536 + gp * 128:1536 + (gp + 1) * 128], rhs=xnT[:, kc, :],
                        start=(kc == 0), stop=(kc == KC - 1)) for kc in range(KC)], reads=[r_winq["U"], r_xnT], writes=[r_pa])
                    pas.append((gp, pa, r_pa))
                yield 5
                for gp, pa, r_pa in pas:
                    P.op("act", lambda e, gp=gp, pa=pa: e.activation(out=uT[:, gp, :], in_=pa[:], func=AF.Gelu_apprx_tanh),
                         reads=[r_pa], writes=[r_uT])
                yield 2
            for b0 in (0, 2):
                st_ = []
                for k_, b in enumerate((b0, b0 + 1)):
                    pa, r_pa = Ab.next()
                    P.group("pe", [lambda e, kc=kc, b=b, pa=pa: e.matmul(
                        pa[:], lhsT=xnT[:, kc, b * 128:(b + 1) * 128], rhs=win[:, kc, 2048:2560],
                        start=(kc == 0), stop=(kc == KC - 1)) for kc in range(KC)], reads=[r_winq["G"], r_xnT], writes=[r_pa])
                    vf, r_vf = vgf.next()
                    st_.append((b, pa, r_pa, vf, r_vf, sqs[k_][0], sqs[k_][1], gsss[k_][0], gsss[k_][1]))
                yield 5
                for (b, pa, r_pa, vf, r_vf, sq_, r_sq_, gs_, r_gs_) in st_:
                    P.op("act", lambda e, pa=pa, vf=vf: e.activation(out=vf[:], in_=pa[:], func=AF.Gelu_apprx_tanh),
                         reads=[r_pa], writes=[r_vf])
                yield 3
                for (b, pa, r_pa, vf, r_vf, sq_, r_sq_, gs_, r_gs_) in st_:
                    P.op("dve", lambda e, vf=vf, sq_=sq_: e.tensor_tensor(out=sq_[:], in0=vf[:], in1=vf[:], op=ALU.mult),
                         reads=[r_vf], writes=[r_sq_])
                yield 1
                for (b, pa, r_pa, vf, r_vf, sq_, r_sq_, gs_, r_gs_) in st_:
                    P.op("dve", lambda e, sq_=sq_, gs_=gs_: e.tensor_reduce(
                        out=gs_[:, 0:8], in_=sq_[:].rearrange("p (g d) -> p g d", d=64), axis=AX.X, op=ALU.add),
                        reads=[r_sq_], writes=[r_gs_])
                yield 3
                for (b, pa, r_pa, vf, r_vf, sq_, r_sq_, gs_, r_gs_) in st_:
                    P.op("act", lambda e, gs_=gs_: e.activation(out=gs_[:, 0:8], in_=gs_[:, 0:8], func=AF.Ln, bias=cst[:, 0:1],
                                                                scale=1.0 / 64), reads=[r_cst], writes=[r_gs_])
                yield 2
                for (b, pa, r_pa, vf, r_vf, sq_, r_sq_, gs_, r_gs_) in st_:
                    P.op("act", lambda e, gs_=gs_: e.activation(out=gs_[:, 0:8], in_=gs_[:, 0:8], func=AF.Exp, scale=-0.5),
                         writes=[r_gs_])
                yield 3
                for (b, pa, r_pa, vf, r_vf, sq_, r_sq_, gs_, r_gs_) in st_:
                    P.op("dve", lambda e, vf=vf, sq_=sq_, gs_=gs_: e.tensor_tensor(
                        out=sq_[:].rearrange("p (g d) -> p g d", d=64), in0=vf[:].rearrange("p (g d) -> p g d", d=64),
                        in1=gs_[:, 0:8].unsqueeze(2).broadcast_to([128, 8, 64]), op=ALU.mult),
                        reads=[r_vf, r_gs_], writes=[r_sq_])
                yield 1
                for (b, pa, r_pa, vf, r_vf, sq_, r_sq_, gs_, r_gs_) in st_:
                    P.op("dve", lambda e, b=b, sq_=sq_: e.tensor_tensor(out=vgn[:, b, :], in0=sq_[:], in1=gnb[:], op=ALU.mult),
                         reads=[r_sq_, r_gnb], writes=[r_vgn])
                yield 1
            for gp0 in (0, 2):
                pas = []
                for gp in (gp0, gp0 + 1):
                    pa, r_pa = Ab.next()
                    fns = []
                    for b in range(4):
                        for gg2 in range(2):
                            g = 2 * gp + gg2
                            fns.append(lambda e, b=b, gg2=gg2, g=g, pa=pa: e.matmul(
                                pa[gg2 * 64:(gg2 + 1) * 64, b * 128:(b + 1) * 128], lhsT=vgn[:, b, g * 64:(g + 1) * 64],
                                rhs=wspT[:, g, :], start=True, stop=True))
                    P.group("pe", fns, reads=[r_vgn, r_wspT], writes=[r_pa])
                    pas.append((gp, pa, r_pa))
                yield 4
                for k_, (gp, pa, r_pa) in enumerate(pas):
                    gt_, r_gt_ = gtmps[k_]
                    P.op("dve", lambda e, gp=gp, pa=pa, gt_=gt_: e.tensor_tensor(
                        out=gt_[:].rearrange("p (b t) -> p b t", t=128), in0=pa[:].rearrange("p (b t) -> p b t", t=128),
                        in1=bS[:, gp, :].unsqueeze(1).broadcast_to([128, 4, 128]), op=ALU.add),
                        reads=[r_pa, r_bS], writes=[r_gt_])
                yield 1
                for k_, (gp, pa, r_pa) in enumerate(pas):
                    gt_, r_gt_ = gtmps[k_]
                    P.op("dve", lambda e, gp=gp, gt_=gt_: e.tensor_tensor(out=sgT[:, gp, :], in0=gt_[:], in1=uT[:, gp, :], op=ALU.mult),
                         reads=[r_gt_, r_uT], writes=[r_sgT])
                yield 1

        def advance(gen, k):
            if gen is None:
                return
            for _ in range(k):
                try:
                    next(gen)
                except StopIteration:
                    return

        N_UNITS = 37

        def att_setup(T):
            n = 4 * T + 4
            Qa, r_Qa = Qas[T % 2]
            nb, r_nb = nbs[T % 2]
            kw_tok, vw_tok = kw_toks[T], vw_toks[T]
            P.op("dve", lambda e: e.tensor_tensor(
                out=nb[:, 0:n, :], in0=Aall[:, 0:n, :],
                in1=carryB[:, 4 * T, :].unsqueeze(1).broadcast_to([128, n, H]), op=ALU.subtract),
                reads=[r_Aall, r_carry], writes=[r_nb])
            items = []
            chunks = []
            for h in range(H):
                for c0 in range(0, n, CH):
                    c1 = min(n, c0 + CH)
                    chunks.append([h, c0, c1, None])
                    for j in range(c0, c1):
                        items.append((h, j, j - c0, len(chunks) - 1))
            last_item_of_chunk = {}
            for idx, it in enumerate(items):
                last_item_of_chunk[it[3]] = idx
            dma_state = {"next": 0}

            def issue_slab():
                ci = dma_state["next"]
                if ci >= len(chunks):
                    return
                dma_state["next"] += 1
                h, c0, c1, _ = chunks[ci]
                (ka, r_ka), (va, r_va) = ksl.next(), vsl.next()
                P.dma("sp", f"ld_k{(ksl.i - 1) % NSB}", lambda e: e.dma_start(
                    out=ka[0:64, 0:(c1 - c0) * 128], in_=kT_d[h, :, c0 * 128:c1 * 128]), writes=[r_ka], extra=[kw_tok])
                P.dma("sp", f"ld_v{(vsl.i - 1) % NSB}", lambda e: e.dma_start(
                    out=va[:, 0:c1 - c0, :], in_=v_d[:, h, c0:c1, :]), writes=[r_va], extra=[vw_tok])
                chunks[ci][3] = (ka, r_ka, va, r_va)

            for _ in range(NSB):
                issue_slab()
            return (n, Qa, r_Qa, nb, r_nb, items, chunks, last_item_of_chunk, issue_slab)

        def attention(T, gen, state):
            n, Qa, r_Qa, nb, r_nb, items, chunks, last_item_of_chunk, issue_slab = state
            attT, r_attT = attTs[T % 2]
            NI = len(items)
            sinfo = [None] * NI
            ob = {}

            def qk(idx):
                h, j, jl, ci = items[idx]
                ka, r_ka, va, r_va = chunks[ci][3]
                sbk, r_sb = Sb.next()
                qlo = max(0, j - 4 * T) * 128
                diag = j >= 4 * T
                fns = [lambda e: e.matmul(sbk[:, qlo:TQ], lhsT=ka[0:65, jl * 128:(jl + 1) * 128], rhs=Qa[0:65, h, qlo:TQ],
                                          start=True, stop=not diag)]
                if diag:
                    fns.append(lambda e: e.matmul(sbk[:, qlo:qlo + 128], lhsT=idn[:], rhs=mneg[:], start=False, stop=True))
                P.group("pe", fns, reads=[r_ka, r_Qa, r_idn, r_mneg], writes=[r_sb])
                pt, r_pt = Pt.next()
                P.op("act", lambda e: e.activation(out=pt[:, qlo:TQ], in_=sbk[:, qlo:TQ], func=AF.Exp,
                                                   bias=nb[:, j, h:h + 1], scale=1.0),
                     reads=[r_sb, r_nb], writes=[r_pt])
                sinfo[idx] = (pt, r_pt, qlo)

            def pv(idx):
                h, j, jl, ci = items[idx]
                ka, r_ka, va, r_va = chunks[ci][3]
                pt, r_pt, qlo = sinfo[idx]
                if j == 0:
                    ob[h] = Ob.next()
                o, r_o = ob[h]
                first, last = (j == 0), (j == n - 1)
                P.op("pe", lambda e: e.matmul(o[:, qlo:TQ], lhsT=va[:, jl, :], rhs=pt[:, qlo:TQ],
                                              start=first, stop=last), reads=[r_va, r_pt], writes=[r_o])
                if last:
                    rc, r_rc = gtmps[1]
                    P.op("dve", lambda e: e.reciprocal(out=rc[64:128, :], in_=o[64:128, :]), reads=[r_o], writes=[r_rc])
                    hf = (h % 2) * 64
                    P.op("dve", lambda e: e.tensor_tensor(out=attT[hf:hf + 64, h // 2, :], in0=o[0:64, :], in1=rc[64:128, :],
                                                          op=ALU.mult), reads=[r_o, r_rc], writes=[r_attT])

            gstate = {"gen": gen, "wait": 0}

            def tick():
                if gstate["gen"] is None:
                    return
                if gstate["wait"] > 0:
                    gstate["wait"] -= 1
                    return
                try:
                    gstate["wait"] = next(gstate["gen"]) or 0
                except StopIteration:
                    gstate["gen"] = None

            for idx in range(NI + 2):
                if idx < NI:
                    qk(idx)
                if idx >= 2:
                    pv(idx - 2)
                    if last_item_of_chunk[items[idx - 2][3]] == idx - 2:
                        issue_slab()
                tick()
            advance(gstate["gen"], 100000)

        def post(T):
            sgT, r_sgT = sgTs[T % 2]
            attT, r_attT = attTs[T % 2]
            for b in range(4):
                xo_, r_xo = xo.next()
                P.dma("sp", f"ld_xo{(xo.i - 1) % 2}", lambda e, b=b, xo_=xo_: e.dma_start(
                    out=xo_[:], in_=x_d[T * TQ + b * 128:T * TQ + (b + 1) * 128, :]), writes=[r_xo])
                yield 4
                pas = []
                for c in range(2):
                    pa, r_pa = Ab.next()
                    fns = [lambda e, hp=hp, b=b, c=c, pa=pa: e.matmul(
                        pa[:], lhsT=attT[:, hp, b * 128:(b + 1) * 128], rhs=woa[:, hp, c * 512:(c + 1) * 512],
                        start=(hp == 0), stop=False) for hp in range(4)]
                    fns += [lambda e, gp=gp, b=b, c=c, pa=pa: e.matmul(
                        pa[:], lhsT=sgT[:, gp, b * 128:(b + 1) * 128], rhs=wos[:, gp, c * 512:(c + 1) * 512],
                        start=False, stop=(gp == 3)) for gp in range(4)]
                    P.group("pe", fns, reads=[r_attT, r_sgT, r_woa, r_wos], writes=[r_pa])
                    pas.append((c, pa, r_pa))
                yield 5
                for c, pa, r_pa in pas:
                    P.op("dve", lambda e, c=c, pa=pa, xo_=xo_: e.tensor_tensor(
                        out=xo_[:, c * 512:(c + 1) * 512], in0=pa[:], in1=xo_[:, c * 512:(c + 1) * 512], op=ALU.add),
                        reads=[r_pa], writes=[r_xo])
                yield 2
                h1_tokens.append(P.dma("sp", f"st_h1{(xo.i - 1) % 2}", lambda e, b=b, xo_=xo_: e.dma_start(
                    out=out_d[T * TQ + b * 128:T * TQ + (b + 1) * 128, :], in_=xo_[:]), reads=[r_xo]))

        g0 = pre(0)
        advance(g0, 1000)
        r_wupd, r_wdnd = Res(), Res()
        for kc in range(KC):
            P.dma("pool", "w_upd", lambda e, kc=kc: e.dma_start(out=wupb_d[kc * 128:(kc + 1) * 128, :],
                                                                 in_=wup_d[kc * 128:(kc + 1) * 128, :],
                                                                 max_dma_last_dim=4096), writes=[r_wupd], extra=[kw_toks[0]])
        for i in range(0, NPAIR, 2):
            P.dma("pool", "w_dnd", lambda e, i=i: e.dma_start(out=wdnb_d[i * 128:(i + 2) * 128, :],
                                                               in_=wdn_d[i * 128:(i + 2) * 128, :],
                                                               max_dma_last_dim=4096), writes=[r_wdnd], extra=[kw_toks[0]])


        def chain(gens):
            for g_ in gens:
                for v_ in g_:
                    yield v_

        state = att_setup(0)
        pending_post = None
        for T in range(NT):
            gens = []
            if pending_post is not None:
                gens.append(pending_post)
            if T + 1 < NT:
                gens.append(pre(T + 1))
            attention(T, chain(gens) if gens else None, state)
            if T + 1 < NT:
                state = att_setup(T + 1)
            pending_post = post(T)
        advance(pending_post, 100000)

        wtoks = [r_wupd.w, r_wdnd.w]
        P.emit(final_waits=h1_tokens[-4:] + wtoks, gate_inc=gate)

    if only_pass1:
        gate_cm.__exit__(None, None, None)
        return nc
    with ExitStack() as es:
        def sb(name, shape, dt):
            return es.enter_context(nc.sbuf_tensor(name, shape, dt))

        def ps(name, shape, dt):
            return es.enter_context(nc.psum_tensor(name, shape, dt))

        P = Prog(nc)
        wup = sb("wup", [128, KC, 2 * DFF], BF16); r_wupq = [Res() for _ in range(4)]
        wdn = sb("wdn", [128, NPAIR, D], BF16); r_wdn = Res()
        xt = sb("xt2", [128, 5, D], F32); r_xt = [Res() for _ in range(5)]
        xnb = Rot([sb(f"xnb2{i}", [128, D], BF16) for i in range(2)])
        xnT = sb("xnT2", [128, KC, TQ], BF16); r_xnT = Res()
        mT = sb("mT", [128, NPAIR, TQ], BF16); r_mT = [Res() for _ in range(NPAIR)]
        ya = Rot([sb(f"ya{i}", [128, TQ], F32) for i in range(3)])
        yg = Rot([sb(f"yg{i}", [128, TQ], F32) for i in range(3)])
        sgl = Rot([sb(f"sgl{i}", [128, TQ], F32) for i in range(1)])
        carry = sb("carry", [128, 2, 2 * NPAIR, 2], F32); r_cy = [[Res() for _ in range(2 * NPAIR)] for _ in range(2)]
        cwt = sb("cwt", [128, 2 * NPAIR, 4], F32); r_cwt = Res()
        g2b = sb("g2b", [128, KC], F32); r_g2b = Res()
        g3b = sb("g3b", [128, D], F32); r_g3b = Res()
        idn = sb("idn2", [128, 128], BF16); r_idn = Res()
        onesf = sb("onesf2", [128, 128], F32); r_onesf = Res()
        cst = sb("cst2", [128, 2], F32); r_cst = Res()
        stt = (sb("ssq2", [128, 4], F32), sb("lnv2", [128, 4], F32), sb("rstd2", [128, 4], F32), [Res() for _ in range(4)])
        st3 = (sb("ssq3", [128, 4], F32), sb("lnv3", [128, 4], F32), sb("rstd3", [128, 4], F32), [Res() for _ in range(4)])
        trb = ps("trb2", [128, 1024], BF16); r_trb = Res()
        pbs = [ps(f"pc{i}", [128, 512], F32) for i in range(7)]
        Ab = Rot(pbs[0:5])
        Db = Rot(pbs[5:7])

        P.op("pool", lambda e: e.memset(onesf[:], 1.0), writes=[r_onesf])
        P.op("pool", lambda e: e.memset(cst[:, 0:1], EPS), writes=[r_cst])
        P.op("pool", lambda e: e.memset(carry[:], 0.0), writes=r_cy[0] + r_cy[1])
        P.op("pool", lambda e: e.affine_select(out=idn[:], in_=onesf[:], pattern=[[-1, 128]], compare_op=ALU.is_equal,
                                               fill=0.0, base=0, channel_multiplier=1), reads=[r_onesf], writes=[r_idn])
        P.dma("sp", "c_g2", lambda e: e.dma_start(out=g2b[:], in_=g2_d.rearrange("(k p) -> p k", p=128), allow_slow_non_contiguous=True), writes=[r_g2b])
        P.dma("sp", "c_g3", lambda e: e.dma_start(out=g3b[:], in_=g3_d.partition_broadcast(128)), writes=[r_g3b])
        for w in range(3):
            P.dma("sp", "c_cw", lambda e, w=w: e.dma_start(out=cwt[:, :, w], in_=cw_d[w].rearrange("(c p) -> p c", p=128),
                                                            allow_slow_non_contiguous=True), writes=[r_cwt])
        P.dma("sp", "c_cw", lambda e: e.dma_start(out=cwt[:, :, 3], in_=cb_d.rearrange("(c p) -> p c", p=128),
                                                  allow_slow_non_contiguous=True), writes=[r_cwt])
        wup_src = wupb_d.rearrange("(k p) n -> p k n", p=128)
        for q in range(4):
            c0, c1 = q * 768, min((q + 1) * 768, DFF)
            for off in (0, DFF):
                P.dma("sp", f"w_up{q}", lambda e, c0=c0, c1=c1, off=off: e.dma_start(
                    out=wup[:, :, off + c0:off + c1], in_=wup_src[:, :, off + c0:off + c1]), writes=[r_wupq[q]])
        P.dma("sp", "w_dn", lambda e: e.dma_start(out=wdn[:], in_=wdnb_d.rearrange("(c p) n -> p c n", p=128)), writes=[r_wdn])

        out_toks = [None] * 4
        all_out = []
        ssq2, lnv2, rstd2, r_c2 = stt

        def norm_block(T, b):
            sl = (4 * T + b) % 5
            P.dma("sp", f"ld_h{sl}", lambda e: e.dma_start(
                out=xt[:, sl, :], in_=out_d[T * TQ + b * 128:T * TQ + (b + 1) * 128, :]), writes=[r_xt[sl]])
            rc_ = r_c2[b]
            xb, r_xb = xnb.next()
            P.op("act", lambda e: e.activation(out=xb[:], in_=xt[:, sl, :], func=AF.Square,
                                               accum_out=ssq2[:, b:b + 1]), reads=[r_xt[sl]], writes=[r_xb, rc_])
            P.op("act", lambda e: e.activation(out=lnv2[:, b:b + 1], in_=ssq2[:, b:b + 1], func=AF.Ln, bias=cst[:, 0:1],
                                               scale=1.0 / D), reads=[r_cst], writes=[rc_])
            P.op("act", lambda e: e.activation(out=rstd2[:, b:b + 1], in_=lnv2[:, b:b + 1], func=AF.Exp, scale=-0.5),
                 writes=[rc_])
            P.op("dve", lambda e: e.tensor_scalar(out=xb[:], in0=xt[:, sl, :], scalar1=rstd2[:, b:b + 1],
                                                  scalar2=None, op0=ALU.mult), reads=[r_xt[sl], rc_], writes=[r_xb])
            return xb, r_xb

        def transp_block(b, xb, r_xb):
            P.group("pe", [lambda e, kc=kc: e.transpose(trb[:, kc * 128:(kc + 1) * 128],
                                                        xb[:, kc * 128:(kc + 1) * 128], idn[:])
                           for kc in range(KC)], reads=[r_xb, r_idn], writes=[r_trb])
            P.op("dve", lambda e: e.tensor_tensor(out=xnT[:, :, b * 128:(b + 1) * 128],
                                                  in0=trb[:].rearrange("p (k t) -> p k t", t=128),
                                                  in1=g2b[:, :].unsqueeze(2).broadcast_to([128, KC, 128]), op=ALU.mult),
                 reads=[r_trb, r_g2b], writes=[r_xnT])

        for b in range(4):
            xb_, r_xb_ = norm_block(0, b)
            transp_block(b, xb_, r_xb_)
        for T in range(NT):
            pend = {}
            cur, nxt = T % 2, (T + 1) % 2
            prev_info = None
            prev2_info = None

            def finish(pinfo):
                ip, infos = pinfo
                (a_y, r_a), (g_y, r_g) = [(x[3], x[4]) for x in infos]
                sg, r_sg = sgl.next()
                P.op("act", lambda e: e.activation(out=sg[:], in_=g_y[:], func=AF.Silu), reads=[r_g], writes=[r_sg])
                P.op("pool", lambda e: e.tensor_tensor(out=mT[:, ip, :], in0=sg[:], in1=a_y[:], op=ALU.mult),
                     reads=[r_sg, r_a], writes=[r_mT[ip]])

            def fix1(x, cur=cur):
                ci, pa, r_pa, y, r_y = x
                P.op("dve", lambda e: e.scalar_tensor_tensor(
                    out=y[:, 0:2], in0=carry[:, cur, ci, 0:2], scalar=cwt[:, ci, 0:1], in1=y[:, 0:2], op0=ALU.mult, op1=ALU.add),
                    reads=[r_cy[cur][ci], r_cwt], writes=[r_y])

            def fix2(x, cur=cur):
                ci, pa, r_pa, y, r_y = x
                P.op("dve", lambda e: e.scalar_tensor_tensor(
                    out=y[:, 0:1], in0=carry[:, cur, ci, 1:2], scalar=cwt[:, ci, 1:2], in1=y[:, 0:1], op0=ALU.mult, op1=ALU.add),
                    reads=[r_cy[cur][ci], r_cwt], writes=[r_y])

            for i in range(NPAIR):
                info = []
                for typ in range(2):
                    ci = typ * NPAIR + i
                    col = ci * 128
                    pa, r_pa = Ab.next()
                    P.group("pe", [lambda e, kc=kc, col=col, pa=pa: e.matmul(
                        pa[:], lhsT=wup[:, kc, col:col + 128], rhs=xnT[:, kc, :],
                        start=(kc == 0), stop=(kc == KC - 1)) for kc in range(KC)], reads=[r_wupq[i // 6], r_xnT], writes=[r_pa])
                    y, r_y = (ya if typ == 0 else yg).next()
                    P.op("act", lambda e, ci=ci, pa=pa, y=y: e.activation(
                        out=y[:], in_=pa[:], func=AF.Identity, scale=cwt[:, ci, 2:3], bias=cwt[:, ci, 3:4]),
                        reads=[r_pa, r_cwt], writes=[r_y])
                    info.append((ci, pa, r_pa, y, r_y))
                if prev2_info is not None:
                    finish(prev2_info)
                    prev2_info = None
                pv_ = prev_info[1] if prev_info is not None else None
                for k_, (ci, pa, r_pa, y, r_y) in enumerate(info):
                    P.op("dve", lambda e, ci=ci, pa=pa, y=y: e.scalar_tensor_tensor(
                        out=y[:, 1:TQ], in0=pa[:, 0:TQ - 1], scalar=cwt[:, ci, 1:2], in1=y[:, 1:TQ], op0=ALU.mult, op1=ALU.add),
                        reads=[r_pa, r_cwt], writes=[r_y])
                    if pv_ is not None:
                        fix1(pv_[k_])
                for k_, (ci, pa, r_pa, y, r_y) in enumerate(info):
                    P.op("dve", lambda e, ci=ci, pa=pa, y=y: e.scalar_tensor_tensor(
                        out=y[:, 2:TQ], in0=pa[:, 0:TQ - 2], scalar=cwt[:, ci, 0:1], in1=y[:, 2:TQ], op0=ALU.mult, op1=ALU.add),
                        reads=[r_pa, r_cwt], writes=[r_y])
                    if pv_ is not None:
                        fix2(pv_[k_])
                for (ci, pa, r_pa, y, r_y) in info:
                    P.op("dve", lambda e, ci=ci, pa=pa, nxt=nxt: e.tensor_copy(out=carry[:, nxt, ci, :], in_=pa[:, TQ - 2:TQ]),
                         reads=[r_pa], writes=[r_cy[nxt][ci]])
                prev2_info = prev_info
                prev_info = (i, info)
            if prev2_info is not None:
                finish(prev2_info)
            for x in prev_info[1]:
                fix1(x)
            for x in prev_info[1]:
                fix2(x)
            finish(prev_info)
            if T + 1 < NT:
                pend[0] = norm_block(T + 1, 0)
            jk, r_jk = yg.items[0]
            jkb = jk[:].bitcast(BF16)
            for b in range(4):
                sl = (4 * T + b) % 5
                dbanks = [Db.next(), Db.next()]
                for c in range(2):
                    pa, r_pa = dbanks[c]
                    P.group("pe", [lambda e, i=i, b=b, c=c, pa=pa: e.matmul(
                        pa[:], lhsT=mT[:, i, b * 128:(b + 1) * 128], rhs=wdn[:, i, c * 512:(c + 1) * 512],
                        start=(i == 0), stop=False) for i in range(16)], reads=r_mT[0:16] + [r_wdn], writes=[r_pa])
                if T + 1 < NT and b >= 1:
                    transp_block(b - 1, *pend[b - 1])
                for c in range(2):
                    pa, r_pa = dbanks[c]
                    P.group("pe", [lambda e, i=i, b=b, c=c, pa=pa: e.matmul(
                        pa[:], lhsT=mT[:, i, b * 128:(b + 1) * 128], rhs=wdn[:, i, c * 512:(c + 1) * 512],
                        start=False, stop=(i == NPAIR - 1)) for i in range(16, NPAIR)], reads=r_mT[16:] + [r_wdn], writes=[r_pa])
                    P.op("dve", lambda e, sl=sl, c=c, pa=pa: e.tensor_tensor(
                        out=xt[:, sl, c * 512:(c + 1) * 512], in0=pa[:], in1=xt[:, sl, c * 512:(c + 1) * 512], op=ALU.add),
                        reads=[r_pa], writes=[r_xt[sl]])
                ssq, lnv, rstd, r_c3 = st3
                rc_ = r_c3[b]
                P.op("act", lambda e, b=b, sl=sl: e.activation(out=jkb, in_=xt[:, sl, :], func=AF.Square,
                                                              accum_out=ssq[:, b:b + 1]), reads=[r_xt[sl]], writes=[r_jk, rc_])
                P.op("act", lambda e, b=b: e.activation(out=lnv[:, b:b + 1], in_=ssq[:, b:b + 1], func=AF.Ln, bias=cst[:, 0:1],
                                                        scale=1.0 / D), reads=[r_cst], writes=[rc_])
                P.op("act", lambda e, b=b: e.activation(out=rstd[:, b:b + 1], in_=lnv[:, b:b + 1], func=AF.Exp, scale=-0.5),
                     writes=[rc_])
                P.op("dve", lambda e, b=b, sl=sl: e.scalar_tensor_tensor(out=xt[:, sl, :], in0=xt[:, sl, :], scalar=rstd[:, b:b + 1],
                                                                        in1=g3b[:], op0=ALU.mult, op1=ALU.mult),
                     reads=[rc_, r_g3b], writes=[r_xt[sl]])
                out_toks[b] = P.dma("sp", f"st_o{sl}", lambda e, T=T, b=b, sl=sl: e.dma_start(
                    out=out_d[T * TQ + b * 128:T * TQ + (b + 1) * 128, :], in_=xt[:, sl, :]), reads=[r_xt[sl]])
                all_out.append(out_toks[b])
                if T + 1 < NT and b + 1 < 4:
                    pend[b + 1] = norm_block(T + 1, b + 1)
            if T + 1 < NT:
                transp_block(3, *pend[3])
        P.emit(final_waits=all_out[-8:], gate_wait=(gate, 1))
    gate_cm.__exit__(None, None, None)
    return nc


_NC_CACHE = {}


def _get_nc(S):
    if S not in _NC_CACHE:
        _NC_CACHE[S] = build_nc(S)
    return _NC_CACHE[S]


def make_in_maps(inputs, B, S):
    f = lambda a: np.ascontiguousarray(np.asarray(a, dtype=np.float32))
    shared = {
        "norm_mix_g": f(inputs["norm_mix_g"]).reshape(D),
        "w_in": f(inputs["w_in"]).reshape(D, IN_COLS),
        "b_forget": f(inputs["b_forget"]).reshape(H),
        "gmlp_norm_g": f(inputs["gmlp_norm_g"]).reshape(512),
        "w_spatial": f(inputs["w_spatial"]).reshape(8, 128, 128),
        "b_spatial": f(inputs["b_spatial"]).reshape(8, 128),
        "w_out": f(inputs["w_out"]).reshape(D, D),
        "norm_ffn_g": f(inputs["norm_ffn_g"]).reshape(D),
        "w_up": f(inputs["w_up"]).reshape(D, 2 * DFF),
        "conv_w": f(inputs["conv_w"]).reshape(3, 2 * DFF),
        "conv_b": f(inputs["conv_b"]).reshape(2 * DFF),
        "w_down": f(inputs["w_down"]).reshape(DFF, D),
        "norm_final_g": f(inputs["norm_final_g"]).reshape(D),
    }
    x = f(inputs["x"])
    return [dict(shared, x=np.ascontiguousarray(x[b])) for b in range(B)]


def kernel(**inputs):
    x = np.asarray(inputs["x"])
    B, S, _ = x.shape
    nc = _get_nc(S)
    in_maps = make_in_maps(inputs, B, S)
    res = run_bass_kernel_spmd(nc, in_maps, core_ids=list(range(B)))
    return np.stack([np.asarray(r["out"]) for r in res.results], axis=0).astype(np.float32)
```

```python
import numpy as np
from contextlib import ExitStack
import concourse.bass as bass
import concourse.mybir as mybir
from concourse.bass_utils import run_bass_kernel_spmd

F32 = mybir.dt.float32
BF16 = mybir.dt.bfloat16
AF = mybir.ActivationFunctionType
ALU = mybir.AluOpType
AX = mybir.AxisListType

D = 1024
KC = 8
H = 8
DFF = 2816
NPAIR = DFF // 128
IN_COLS = 2568
EPS = 1e-6
TQ = 512
CH = 16
NSB = 3
ENG_NAMES = ["pe", "act", "dve", "pool", "sp"]


class Res:
    def __init__(self):
        self.w = None
        self.r = []


class Prog:
    def __init__(self, nc):
        self.nc = nc
        self.ops = {e: [] for e in ENG_NAMES}
        self.cnt = {}
        self.sem_keys = []

    def _bump(self, key, n):
        if key not in self.cnt:
            self.cnt[key] = 0
            self.sem_keys.append(key)
        self.cnt[key] += n
        return (key, self.cnt[key])

    def _deps(self, reads, writes, extra):
        w = [t for t in extra if t is not None]
        for r in reads:
            if r.w is not None:
                w.append(r.w)
        for r in writes:
            if r.w is not None:
                w.append(r.w)
            w.extend(r.r)
        return w

    def _done(self, tok, reads, writes):
        for r in reads:
            r.r.append(tok)
        for r in writes:
            r.w = tok
            r.r = []

    def op(self, eng, fn, reads=(), writes=(), extra=()):
        waits = self._deps(reads, writes, extra)
        tok = self._bump(eng, 1)
        self.ops[eng].append((fn, waits, (eng, 1)))
        self._done(tok, reads, writes)
        return tok

    def group(self, eng, fns, reads=(), writes=(), extra=()):
        waits = self._deps(reads, writes, extra)
        for i, fn in enumerate(fns):
            last = i == len(fns) - 1
            if last:
                tok = self._bump(eng, 1)
            self.ops[eng].append((fn, waits if i == 0 else [], (eng, 1) if last else None))
        self._done(tok, reads, writes)
        return tok

    def dma(self, eng, semkey, fn, reads=(), writes=(), extra=()):
        waits = self._deps(reads, writes, extra)
        tok = self._bump(semkey, 16)
        self.ops[eng].append((fn, waits, (semkey, 16)))
        self._done(tok, reads, writes)
        return tok

    def emit(self, final_waits=(), gate_wait=None, gate_inc=None):
        nc = self.nc
        with ExitStack() as es:
            sems = {}
            for k in self.sem_keys:
                sems[k] = es.enter_context(nc.semaphore("s_" + k))
            block = es.enter_context(nc.Block())
            engs = {"pe": block.tensor, "act": block.scalar, "dve": block.vector,
                    "pool": block.gpsimd, "sp": block.sync}
            for en in ENG_NAMES:
                ops = self.ops[en]
                fw = [t for t in final_waits if t is not None] if en == "sp" else []
                if not ops and not fw:
                    continue

                def body(e, ops=ops, fw=fw, en=en):
                    seen = {}
                    if gate_wait is not None:
                        e.wait_ge(gate_wait[0], gate_wait[1])
                    for fn, waits, inc in ops:
                        for (k, v) in waits:
                            if seen.get(k, 0) < v:
                                e.wait_ge(sems[k], v)
                                seen[k] = v
                        ins = fn(e)
                        if inc is not None:
                            ins.then_inc(sems[inc[0]], inc[1])
                    for (k, v) in fw:
                        if seen.get(k, 0) < v:
                            e.wait_ge(sems[k], v)
                            seen[k] = v
                    if gate_inc is not None and en == "sp":
                        e.sem_inc(gate_inc, 1)
                engs[en](body)


class Rot:
    def __init__(self, items):
        self.items = [(a, Res()) for a in items]
        self.i = 0

    def next(self):
        it = self.items[self.i % len(self.items)]
        self.i += 1
        return it


def _norm_transpose(P, xt, r_xt, gbc, r_gbc, xnb, xnT, r_xnT, trb, r_trb, idn, r_idn, cst, r_cst, junk, r_junk, st):
    ssq, lnv, rstd, r_cols = st
    for b in range(4):
        rc_ = r_cols[b]
        P.op("act", lambda e, b=b: e.activation(out=junk[:], in_=xt[:, b, :], func=AF.Square,
                                                accum_out=ssq[:, b:b + 1]),
             reads=[r_xt[b]], writes=[r_junk, rc_])
        P.op("act", lambda e, b=b: e.activation(out=lnv[:, b:b + 1], in_=ssq[:, b:b + 1], func=AF.Ln, bias=cst[:, 0:1],
                                                scale=1.0 / D), reads=[r_cst], writes=[rc_])
        P.op("act", lambda e, b=b: e.activation(out=rstd[:, b:b + 1], in_=lnv[:, b:b + 1], func=AF.Exp, scale=-0.5),
             writes=[rc_])
    for b in range(4):
        rc_ = r_cols[b]
        xb, r_xb = xnb.next()
        P.op("dve", lambda e, b=b, xb=xb: e.tensor_scalar(out=xb[:], in0=xt[:, b, :], scalar1=rstd[:, b:b + 1],
                                                          scalar2=None, op0=ALU.mult),
             reads=[r_xt[b], rc_], writes=[r_xb])
        P.group("pe", [lambda e, kc=kc, xb=xb: e.transpose(trb[:, kc * 128:(kc + 1) * 128],
                                                            xb[:, kc * 128:(kc + 1) * 128], idn[:])
                       for kc in range(KC)], reads=[r_xb, r_idn], writes=[r_trb])
        P.op("dve", lambda e, b=b: e.tensor_tensor(out=xnT[:, :, b * 128:(b + 1) * 128],
                                                   in0=trb[:].rearrange("p (k t) -> p k t", t=128),
                                                   in1=gbc[:, :].unsqueeze(2).broadcast_to([128, KC, 128]), op=ALU.mult),
             reads=[r_trb, r_gbc], writes=[r_xnT])


def build_nc(S=8192, only_pass1=False):
    assert S % TQ == 0
    NT = S // TQ
    NBLK = S // 128
    nc = bass.Bass("TRN2", target_bir_lowering=False)

    def din(name, shape):
        return nc.dram_tensor(name, shape, F32, kind="ExternalInput").ap()

    x_d = din("x", [S, D])
    g1_d = din("norm_mix_g", [D])
    win_d = din("w_in", [D, IN_COLS])
    bf_d = din("b_forget", [H])
    gg_d = din("gmlp_norm_g", [512])
    wsp_d = din("w_spatial", [8, 128, 128])
    bsp_d = din("b_spatial", [8, 128])
    wout_d = din("w_out", [D, D])
    g2_d = din("norm_ffn_g", [D])
    wup_d = din("w_up", [D, 2 * DFF])
    cw_d = din("conv_w", [3, 2 * DFF])
    cb_d = din("conv_b", [2 * DFF])
    wdn_d = din("w_down", [DFF, D])
    g3_d = din("norm_final_g", [D])
    out_d = nc.dram_tensor("out", [S, D], F32, kind="ExternalOutput").ap()
    kT_d = nc.dram_tensor("kT_scr", [H, 64, S], BF16, kind="Internal").ap()
    v_d = nc.dram_tensor("v_scr", [128, H, NBLK, 128], BF16, kind="Internal").ap()
    wupb_d = nc.dram_tensor("wup_scr", [D, 2 * DFF], BF16, kind="Internal").ap()
    wdnb_d = nc.dram_tensor("wdn_scr", [DFF, D], BF16, kind="Internal").ap()

    h1_tokens = []
    gate_cm = nc.semaphore("gate")
    gate = gate_cm.__enter__()

    with ExitStack() as es:
        def sb(name, shape, dt):
            return es.enter_context(nc.sbuf_tensor(name, shape, dt))

        def ps(name, shape, dt):
            return es.enter_context(nc.psum_tensor(name, shape, dt))

        P = Prog(nc)
        win = sb("win", [128, KC, IN_COLS], BF16); r_win = Res()
        woa = sb("woa", [128, 4, D], BF16); r_woa = Res()
        wos = sb("wos", [128, 4, D], BF16); r_wos = Res()
        xs = Rot([sb(f"xs{i}", [128, D], F32) for i in range(2)])
        xo = Rot([sb(f"xo{i}", [128, D], F32) for i in range(2)])
        xnb = Rot([sb(f"xnb{i}", [128, D], BF16) for i in range(2)])
        xnT = sb("xnT", [128, KC, TQ], BF16); r_xnT = Res()
        Qas = [(sb(f"Qa{i}", [65, H, TQ], BF16), Res()) for i in range(2)]
        kst = sb("kst", [64, H, TQ], BF16); r_kst = Res()
        vst = sb("vst", [128, H, 4, 128], BF16); r_vst = Res()
        ksl = Rot([sb(f"ksl{i}", [65, CH * 128], BF16) for i in range(NSB)])
        vsl = Rot([sb(f"vsl{i}", [128, CH, 128], BF16) for i in range(NSB)])
        Pt = Rot([sb(f"Pt{i}", [128, TQ], BF16) for i in range(3)])
        attTs = [(sb(f"attT{i}", [128, 4, TQ], BF16), Res()) for i in range(2)]
        uT = sb("uT", [128, 4, TQ], BF16); r_uT = Res()
        sgTs = [(sb(f"sgT{i}", [128, 4, TQ], BF16), Res()) for i in range(2)]
        vgf = Rot([sb(f"vgf{i}", [128, 512], F32) for i in range(2)])
        sqs = [(sb(f"sq{i}", [128, 512], F32), Res()) for i in range(2)]
        vgn = sb("vgn", [128, 4, 512], BF16); r_vgn = Res()
        gtmps = [(sb(f"gtmp{i}", [128, TQ], F32), Res()) for i in range(2)]
        gbc = sb("gbc", [128, KC], F32); r_gbc = Res()
        gnb = sb("gnb", [128, 512], F32); r_gnb = Res()
        idn = sb("idn", [128, 128], BF16); r_idn = Res()
        onesf = sb("onesf", [128, 128], F32); r_onesf = Res()
        negf = sb("negf", [128, 128], F32); r_negf = Res()
        utri = sb("utri", [128, 128], F32); r_utri = Res()
        mneg = sb("mneg", [128, 128], BF16); r_mneg = Res()
        wspf = sb("wspf", [128, 128], F32); r_wspf = Res()
        wspb = sb("wspb", [128, 128], BF16); r_wspb = Res()
        wspT = sb("wspT", [128, 8, 128], BF16); r_wspT = Res()
        bS = sb("bS", [128, 4, 128], F32); r_bS = Res()
        bfb = sb("bfb", [128, H], F32); r_bfb = Res()
        cst = sb("cst", [128, 2], F32); r_cst = Res()
        stt = (sb("ssq", [128, 4], F32), sb("lnv", [128, 4], F32), sb("rstd", [128, 4], F32), Res())
        gsss = [(sb(f"gss{i}", [128, 8], F32), Res()) for i in range(2)]
        zf = sb("zf", [128, 4, H], F32); r_zf = Res()
        nl = sb("nl", [128, 4, H], F32); r_nl = Res()
        Aall = sb("Aall", [128, NBLK, H], F32); r_Aall = Res()
        carryB = sb("carryB", [128, NBLK + 1, H], F32); r_carry = Res()
        nbs = [(sb(f"nb{i}", [128, NBLK, H], F32), Res()) for i in range(1)] * 2

        trb = ps("trb", [128, 1024], BF16); r_trb = Res()
        pbs = [ps(f"pb{i}", [128, 512], F32) for i in range(7)]
        Ab = Rot(pbs[0:2])
        Sb = Rot(pbs[2:5])
        Ob = Rot(pbs[5:7])

        P.op("pool", lambda e: e.memset(onesf[:], 1.0), writes=[r_onesf])
        P.op("pool", lambda e: e.memset(negf[:], -30000.0), writes=[r_negf])
        P.op("pool", lambda e: e.memset(cst[:, 0:1], EPS), writes=[r_cst])
        P.op("pool", lambda e: e.memset(cst[:, 1:2], 1.0), writes=[r_cst])
        P.op("pool", lambda e: e.affine_select(out=idn[:], in_=onesf[:], pattern=[[-1, 128]], compare_op=ALU.is_equal,
                                               fill=0.0, base=0, channel_multiplier=1), reads=[r_onesf], writes=[r_idn])
        P.op("pool", lambda e: e.affine_select(out=utri[:], in_=onesf[:], pattern=[[1, 128]], compare_op=ALU.is_ge,
                                               fill=0.0, base=0, channel_multiplier=-1), reads=[r_onesf], writes=[r_utri])
        P.op("pool", lambda e: e.affine_select(out=mneg[:], in_=negf[:], pattern=[[-1, 128]], compare_op=ALU.is_gt,
                                               fill=0.0, base=0, channel_multiplier=1), reads=[r_negf], writes=[r_mneg])
        P.op("pool", lambda e: e.memset(carryB[:, 0, :], 0.0), writes=[r_carry])
        P.op("pool", lambda e: e.memset(vst[:, :, :, 64:128], 1.0), writes=[r_vst])
        for (a, r) in ksl.items:
            P.op("pool", lambda e, a=a: e.memset(a[64:65, :], 1.0), writes=[r])
        P.dma("sp", "c_gbc", lambda e: e.dma_start(out=gbc[:], in_=g1_d.rearrange("(k p) -> p k", p=128), allow_slow_non_contiguous=True), writes=[r_gbc])
        P.dma("sp", "c_gnb", lambda e: e.dma_start(out=gnb[:], in_=gg_d.partition_broadcast(128)), writes=[r_gnb])
        P.dma("sp", "c_bfb", lambda e: e.dma_start(out=bfb[:], in_=bf_d.partition_broadcast(128)), writes=[r_bfb])
        for g in range(8):
            gp, gg2 = g // 2, g % 2
            P.dma("sp", "c_bS", lambda e, g=g, gp=gp, gg2=gg2: e.dma_start(
                out=bS[gg2 * 64:(gg2 + 1) * 64, gp, :], in_=bsp_d[g].partition_broadcast(64)), writes=[r_bS])
        win_src = win_d.rearrange("(k p) n -> p k n", p=128)
        r_winq = {g: Res() for g in "KVQFUG"}
        for g, (c0, c1) in (("K", (512, 1024)), ("V", (1024, 1536)), ("Q", (0, 512)), ("F", (2560, 2568)),
                            ("U", (1536, 2048)), ("G", (2048, 2560))):
            P.dma("pool", "w_in" + g, lambda e, c0=c0, c1=c1: e.dma_start(out=win[:, :, c0:c1], in_=win_src[:, :, c0:c1],
                                                                        max_dma_last_dim=4096), writes=[r_winq[g]])
        P.dma("pool", "w_oa", lambda e: e.dma_start(out=woa[:], in_=wout_d[0:512, :].rearrange("(c p) n -> p c n", p=128),
                                                    max_dma_last_dim=4096), writes=[r_woa])
        P.dma("pool", "w_os", lambda e: e.dma_start(out=wos[:], in_=wout_d[512:1024, :].rearrange("(c p) n -> p c n", p=128),
                                                    max_dma_last_dim=4096), writes=[r_wos])
        for g in range(8):
            P.dma("sp", "c_wsp", lambda e, g=g: e.dma_start(out=wspf[:], in_=wsp_d[g]), writes=[r_wspf])
            P.op("pool", lambda e: e.affine_select(out=wspb[:], in_=wspf[:], pattern=[[-1, 128]], compare_op=ALU.is_ge,
                                                   fill=0.0, base=0, channel_multiplier=1), reads=[r_wspf], writes=[r_wspb])
            P.op("pe", lambda e: e.transpose(trb[:, 0:128], wspb[:], idn[:]), reads=[r_wspb, r_idn], writes=[r_trb])
            P.op("dve", lambda e, g=g: e.tensor_copy(out=wspT[:, g, :], in_=trb[:, 0:128]), reads=[r_trb], writes=[r_wspT])
        kw_toks, vw_toks = {}, {}
        r_cols = [Res() for _ in range(4)]
        ssq, lnv, rstd, _ = stt

        def pre(T):
            n = 4 * T + 4
            Qa, r_Qa = Qas[T % 2]
            nb, r_nb = nbs[T % 2]
            sgT, r_sgT = sgTs[T % 2]
            for b in range(4):
                xs_, r_xs = xs.next()
                P.dma("sp", f"ld_x{(xs.i - 1) % 2}", lambda e, b=b, xs_=xs_: e.dma_start(
                    out=xs_[:], in_=x_d[T * TQ + b * 128:T * TQ + (b + 1) * 128, :]), writes=[r_xs])
                yield 6
                rc_ = r_cols[b]
                xb, r_xb = xnb.next()
                P.op("act", lambda e, b=b, xs_=xs_, xb=xb: e.activation(out=xb[:], in_=xs_[:], func=AF.Square,
                                                                        accum_out=ssq[:, b:b + 1]), reads=[r_xs], writes=[r_xb, rc_])
                yield 2
                P.op("act", lambda e, b=b: e.activation(out=lnv[:, b:b + 1], in_=ssq[:, b:b + 1], func=AF.Ln,
                                                        bias=cst[:, 0:1], scale=1.0 / D), reads=[r_cst], writes=[rc_])
                yield 1
                P.op("act", lambda e, b=b: e.activation(out=rstd[:, b:b + 1], in_=lnv[:, b:b + 1], func=AF.Exp, scale=-0.5),
                     writes=[rc_])
                yield 2
                P.op("dve", lambda e, b=b, xb=xb, xs_=xs_: e.tensor_scalar(out=xb[:], in0=xs_[:], scalar1=rstd[:, b:b + 1],
                                                                           scalar2=None, op0=ALU.mult),
                     reads=[r_xs, rc_], writes=[r_xb])
                yield 3
                P.group("pe", [lambda e, kc=kc, xb=xb: e.transpose(trb[:, kc * 128:(kc + 1) * 128],
                                                                    xb[:, kc * 128:(kc + 1) * 128], idn[:])
                               for kc in range(KC)], reads=[r_xb, r_idn], writes=[r_trb])
                yield 3
                P.op("dve", lambda e, b=b: e.tensor_tensor(out=xnT[:, :, b * 128:(b + 1) * 128],
                                                           in0=trb[:].rearrange("p (k t) -> p k t", t=128),
                                                           in1=gbc[:, :].unsqueeze(2).broadcast_to([128, KC, 128]), op=ALU.mult),
                     reads=[r_trb, r_gbc], writes=[r_xnT])
                yield 2
            for hp0 in (0, 2):
                pas = []
                for hp in (hp0, hp0 + 1):
                    pa, r_pa = Ab.next()
                    P.group("pe", [lambda e, kc=kc, hp=hp, pa=pa: e.matmul(
                        pa[:], lhsT=win[:, kc, 512 + hp * 128:512 + (hp + 1) * 128], rhs=xnT[:, kc, :],
                        start=(kc == 0), stop=(kc == KC - 1)) for kc in range(KC)], reads=[r_winq["K"], r_xnT], writes=[r_pa])
                    pas.append((hp, pa, r_pa))
                yield 5
                for hp, pa, r_pa in pas:
                    P.op("dve", lambda e, hp=hp, pa=pa: e.tensor_copy(out=kst[0:64, 2 * hp, :], in_=pa[0:64, :]),
                         reads=[r_pa], writes=[r_kst])
                    P.op("dve", lambda e, hp=hp, pa=pa: e.tensor_copy(out=kst[0:64, 2 * hp + 1, :], in_=pa[64:128, :]),
                         reads=[r_pa], writes=[r_kst])
                yield 2
            kw_toks[T] = P.dma("sp", "st_k", lambda e: e.dma_start(
                out=kT_d[:, :, T * TQ:(T + 1) * TQ].rearrange("h d t -> d h t"), in_=kst[:]), reads=[r_kst])
            for b0 in (0, 2):
                pas = []
                for b in (b0, b0 + 1):
                    pa, r_pa = Ab.next()
                    P.group("pe", [lambda e, kc=kc, b=b, pa=pa: e.matmul(
                        pa[:], lhsT=xnT[:, kc, b * 128:(b + 1) * 128], rhs=win[:, kc, 1024:1536],
                        start=(kc == 0), stop=(kc == KC - 1)) for kc in range(KC)], reads=[r_winq["V"], r_xnT], writes=[r_pa])
                    pas.append((b, pa, r_pa))
                yield 5
                for b, pa, r_pa in pas:
                    P.op("dve", lambda e, b=b, pa=pa: e.tensor_copy(
                        out=vst[:, :, b, 0:64], in_=pa[:].rearrange("p (h d) -> p h d", d=64)), reads=[r_pa], writes=[r_vst])
                yield 2
            vw_toks[T] = P.dma("sp", "st_v", lambda e: e.dma_start(out=v_d[:, :, 4 * T:4 * T + 4, :], in_=vst[:]),
                               reads=[r_vst])
            for hp0 in (0, 2):
                pas = []
                for hp in (hp0, hp0 + 1):
                    pa, r_pa = Ab.next()
                    P.group("pe", [lambda e, kc=kc, hp=hp, pa=pa: e.matmul(
                        pa[:], lhsT=win[:, kc, hp * 128:(hp + 1) * 128], rhs=xnT[:, kc, :],
                        start=(kc == 0), stop=(kc == KC - 1)) for kc in range(KC)], reads=[r_winq["Q"], r_xnT], writes=[r_pa])
                    pas.append((hp, pa, r_pa))
                yield 5
                for hp, pa, r_pa in pas:
                    P.op("dve", lambda e, hp=hp, pa=pa: e.tensor_scalar(out=Qa[0:64, 2 * hp, :], in0=pa[0:64, :],
                                                                        scalar1=0.125, scalar2=None, op0=ALU.mult),
                         reads=[r_pa], writes=[r_Qa])
                    P.op("dve", lambda e, hp=hp, pa=pa: e.tensor_scalar(out=Qa[0:64, 2 * hp + 1, :], in0=pa[64:128, :],
                                                                        scalar1=0.125, scalar2=None, op0=ALU.mult),
                         reads=[r_pa], writes=[r_Qa])
                yield 2
            pa, r_pa = Ab.next()
            for b in range(4):
                P.group("pe", [lambda e, kc=kc, b=b, pa=pa: e.matmul(
                    pa[:, b * 8:(b + 1) * 8], lhsT=xnT[:, kc, b * 128:(b + 1) * 128], rhs=win[:, kc, 2560:2568],
                    start=(kc == 0), stop=(kc == KC - 1)) for kc in range(KC)], reads=[r_winq["F"], r_xnT], writes=[r_pa])
            yield 4
            P.op("dve", lambda e, pa=pa: e.tensor_tensor(
                out=zf[:], in0=pa[:, 0:32].rearrange("p (b h) -> p b h", h=H),
                in1=bfb[:].unsqueeze(1).broadcast_to([128, 4, H]), op=ALU.add), reads=[r_pa, r_bfb], writes=[r_zf])
            yield 3
            P.op("act", lambda e: e.activation(out=zf[:], in_=zf[:], func=AF.Exp, scale=-1.0), writes=[r_zf])
            yield 2
            P.op("act", lambda e: e.activation(out=nl[:], in_=zf[:], func=AF.Ln, bias=cst[:, 1:2], scale=1.0),
                 reads=[r_zf, r_cst], writes=[r_nl])
            yield 2
            pa, r_pa = Ab.next()
            for b in range(4):
                P.op("pe", lambda e, b=b, pa=pa: e.matmul(pa[:, b * 8:(b + 1) * 8], lhsT=utri[:], rhs=nl[:, b, :],
                                                          start=True, stop=True), reads=[r_utri, r_nl], writes=[r_pa])
                P.op("pe", lambda e, b=b, pa=pa: e.matmul(pa[:, 32 + b * 8:32 + (b + 1) * 8], lhsT=onesf[:], rhs=nl[:, b, :],
                                                          start=True, stop=True), reads=[r_onesf, r_nl], writes=[r_pa])
            yield 4
            for b in range(4):
                blk = 4 * T + b
                P.op("dve", lambda e, b=b, blk=blk, pa=pa: e.tensor_tensor(
                    out=Aall[:, blk, :], in0=pa[:, b * 8:(b + 1) * 8], in1=carryB[:, blk, :], op=ALU.add),
                    reads=[r_pa, r_carry], writes=[r_Aall])
                P.op("dve", lambda e, b=b, blk=blk, pa=pa: e.tensor_tensor(
                    out=carryB[:, blk + 1, :], in0=pa[:, 32 + b * 8:32 + (b + 1) * 8], in1=carryB[:, blk, :], op=ALU.add),
                    reads=[r_pa], writes=[r_carry])
                yield 1
            P.op("dve", lambda e: e.tensor_tensor(
                out=Qa[64:65, :, :].rearrange("p h (b t) -> p h b t", t=128),
                in0=carryB[64:65, 4 * T, :].unsqueeze(2).unsqueeze(3).broadcast_to([1, H, 4, 128]),
                in1=carryB[64:65, 4 * T:4 * T + 4, :].rearrange("p b h -> p h b").unsqueeze(3).broadcast_to([1, H, 4, 128]),
                op=ALU.subtract), reads=[r_carry], writes=[r_Qa])
            yield 1
            for gp0 in (0, 2):
                pas = []
                for gp in (gp0, gp0 + 1):
                    pa, r_pa = Ab.next()
                    P.group("pe", [lambda e, kc=kc, gp=gp, pa=pa: e.matmul(
                        pa[:], lhsT=win[:, kc, 1536 + gp * 128:1536 + (gp + 1) * 128], rhs=xnT[:, kc, :],
                        start=(kc == 0), stop=(kc == KC - 1)) for kc in range(KC)], reads=[r_winq["U"], r_xnT], writes=[r_pa])
                    pas.append((gp, pa, r_pa))
                yield 5
                for gp, pa, r_pa in pas:
                    P.op("act", lambda e, gp=gp, pa=pa: e.activation(out=uT[:, gp, :], in_=pa[:], func=AF.Gelu_apprx_tanh),
                         reads=[r_pa], writes=[r_uT])
                yield 2
            for b0 in (0, 2):
                st_ = []
                for k_, b in enumerate((b0, b0 + 1)):
                    pa, r_pa = Ab.next()
                    P.group("pe", [lambda e, kc=kc, b=b, pa=pa: e.matmul(
                        pa[:], lhsT=xnT[:, kc, b * 128:(b + 1) * 128], rhs=win[:, kc, 2048:2560],
                        start=(kc == 0), stop=(kc == KC - 1)) for kc in range(KC)], reads=[r_winq["G"], r_xnT], writes=[r_pa])
                    vf, r_vf = vgf.next()
                    st_.append((b, pa, r_pa, vf, r_vf, sqs[k_][0], sqs[k_][1], gsss[k_][0], gsss[k_][1]))
                yield 5
                for (b, pa, r_pa, vf, r_vf, sq_, r_sq_, gs_, r_gs_) in st_:
                    P.op("act", lambda e, pa=pa, vf=vf: e.activation(out=vf[:], in_=pa[:], func=AF.Gelu_apprx_tanh),
                         reads=[r_pa], writes=[r_vf])
                yield 3
                for (b, pa, r_pa, vf, r_vf, sq_, r_sq_, gs_, r_gs_) in st_:
                    P.op("dve", lambda e, vf=vf, sq_=sq_: e.tensor_tensor(out=sq_[:], in0=vf[:], in1=vf[:], op=ALU.mult),
                         reads=[r_vf], writes=[r_sq_])
                yield 1
                for (b, pa, r_pa, vf, r_vf, sq_, r_sq_, gs_, r_gs_) in st_:
                    P.op("dve", lambda e, sq_=sq_, gs_=gs_: e.tensor_reduce(
                        out=gs_[:, 0:8], in_=sq_[:].rearrange("p (g d) -> p g d", d=64), axis=AX.X, op=ALU.add),
                        reads=[r_sq_], writes=[r_gs_])
                yield 3
                for (b, pa, r_pa, vf, r_vf, sq_, r_sq_, gs_, r_gs_) in st_:
                    P.op("act", lambda e, gs_=gs_: e.activation(out=gs_[:, 0:8], in_=gs_[:, 0:8], func=AF.Ln, bias=cst[:, 0:1],
                                                                scale=1.0 / 64), reads=[r_cst], writes=[r_gs_])
                yield 2
                for (b, pa, r_pa, vf, r_vf, sq_, r_sq_, gs_, r_gs_) in st_:
                    P.op("act", lambda e, gs_=gs_: e.activation(out=gs_[:, 0:8], in_=gs_[:, 0:8], func=AF.Exp, scale=-0.5),
                         writes=[r_gs_])
                yield 3
                for (b, pa, r_pa, vf, r_vf, sq_, r_sq_, gs_, r_gs_) in st_:
                    P.op("dve", lambda e, vf=vf, sq_=sq_, gs_=gs_: e.tensor_tensor(
                        out=sq_[:].rearrange("p (g d) -> p g d", d=64), in0=vf[:].rearrange("p (g d) -> p g d", d=64),
                        in1=gs_[:, 0:8].unsqueeze(2).broadcast_to([128, 8, 64]), op=ALU.mult),
                        reads=[r_vf, r_gs_], writes=[r_sq_])
                yield 1
                for (b, pa, r_pa, vf, r_vf, sq_, r_sq_, gs_, r_gs_) in st_:
                    P.op("dve", lambda e, b=b, sq_=sq_: e.tensor_tensor(out=vgn[:, b, :], in0=sq_[:], in1=gnb[:], op=ALU.mult),
                         reads=[r_sq_, r_gnb], writes=[r_vgn])
                yield 1
            for gp0 in (0, 2):
                pas = []
                for gp in (gp0, gp0 + 1):
                    pa, r_pa = Ab.next()
                    fns = []
                    for b in range(4):
                        for gg2 in range(2):
                            g = 2 * gp + gg2
                            fns.append(lambda e, b=b, gg2=gg2, g=g, pa=pa: e.matmul(
                                pa[gg2 * 64:(gg2 + 1) * 64, b * 128:(b + 1) * 128], lhsT=vgn[:, b, g * 64:(g + 1) * 64],
                                rhs=wspT[:, g, :], start=True, stop=True))
                    P.group("pe", fns, reads=[r_vgn, r_wspT], writes=[r_pa])
                    pas.append((gp, pa, r_pa))
                yield 4
                for k_, (gp, pa, r_pa) in enumerate(pas):
                    gt_, r_gt_ = gtmps[k_]
                    P.op("dve", lambda e, gp=gp, pa=pa, gt_=gt_: e.tensor_tensor(
                        out=gt_[:].rearrange("p (b t) -> p b t", t=128), in0=pa[:].rearrange("p (b t) -> p b t", t=128),
                        in1=bS[:, gp, :].unsqueeze(1).broadcast_to([128, 4, 128]), op=ALU.add),
                        reads=[r_pa, r_bS], writes=[r_gt_])
                yield 1
                for k_, (gp, pa, r_pa) in enumerate(pas):
                    gt_, r_gt_ = gtmps[k_]
                    P.op("dve", lambda e, gp=gp, gt_=gt_: e.tensor_tensor(out=sgT[:, gp, :], in0=gt_[:], in1=uT[:, gp, :], op=ALU.mult),
                         reads=[r_gt_, r_uT], writes=[r_sgT])
                yield 1

        def advance(gen, k):
            if gen is None:
                return
            for _ in range(k):
                try:
                    next(gen)
                except StopIteration:
                    return

        N_UNITS = 37

        def att_setup(T):
            n = 4 * T + 4
            Qa, r_Qa = Qas[T % 2]
            nb, r_nb = nbs[T % 2]
            kw_tok, vw_tok = kw_toks[T], vw_toks[T]
            P.op("dve", lambda e: e.tensor_tensor(
                out=nb[:, 0:n, :], in0=Aall[:, 0:n, :],
                in1=carryB[:, 4 * T, :].unsqueeze(1).broadcast_to([128, n, H]), op=ALU.subtract),
                reads=[r_Aall, r_carry], writes=[r_nb])
            items = []
            chunks = []
            for h in range(H):
                for c0 in range(0, n, CH):
                    c1 = min(n, c0 + CH)
                    chunks.append([h, c0, c1, None])
                    for j in range(c0, c1):
                        items.append((h, j, j - c0, len(chunks) - 1))
            last_item_of_chunk = {}
            for idx, it in enumerate(items):
                last_item_of_chunk[it[3]] = idx
            dma_state = {"next": 0}

            def issue_slab():
                ci = dma_state["next"]
                if ci >= len(chunks):
                    return
                dma_state["next"] += 1
                h, c0, c1, _ = chunks[ci]
                (ka, r_ka), (va, r_va) = ksl.next(), vsl.next()
                P.dma("sp", f"ld_k{(ksl.i - 1) % NSB}", lambda e: e.dma_start(
                    out=ka[0:64, 0:(c1 - c0) * 128], in_=kT_d[h, :, c0 * 128:c1 * 128]), writes=[r_ka], extra=[kw_tok])
                P.dma("sp", f"ld_v{(vsl.i - 1) % NSB}", lambda e: e.dma_start(
                    out=va[:, 0:c1 - c0, :], in_=v_d[:, h, c0:c1, :]), writes=[r_va], extra=[vw_tok])
                chunks[ci][3] = (ka, r_ka, va, r_va)

            for _ in range(NSB):
                issue_slab()
            return (n, Qa, r_Qa, nb, r_nb, items, chunks, last_item_of_chunk, issue_slab)

        def attention(T, gen, state):
            n, Qa, r_Qa, nb, r_nb, items, chunks, last_item_of_chunk, issue_slab = state
            attT, r_attT = attTs[T % 2]
            NI = len(items)
            sinfo = [None] * NI
            ob = {}

            def qk(idx):
                h, j, jl, ci = items[idx]
                ka, r_ka, va, r_va = chunks[ci][3]
                sbk, r_sb = Sb.next()
                qlo = max(0, j - 4 * T) * 128
                diag = j >= 4 * T
                fns = [lambda e: e.matmul(sbk[:, qlo:TQ], lhsT=ka[0:65, jl * 128:(jl + 1) * 128], rhs=Qa[0:65, h, qlo:TQ],
                                          start=True, stop=not diag)]
                if diag:
                    fns.append(lambda e: e.matmul(sbk[:, qlo:qlo + 128], lhsT=idn[:], rhs=mneg[:], start=False, stop=True))
                P.group("pe", fns, reads=[r_ka, r_Qa, r_idn, r_mneg], writes=[r_sb])
                pt, r_pt = Pt.next()
                P.op("act", lambda e: e.activation(out=pt[:, qlo:TQ], in_=sbk[:, qlo:TQ], func=AF.Exp,
                                                   bias=nb[:, j, h:h + 1], scale=1.0),
                     reads=[r_sb, r_nb], writes=[r_pt])
                sinfo[idx] = (pt, r_pt, qlo)

            def pv(idx):
                h, j, jl, ci = items[idx]
                ka, r_ka, va, r_va = chunks[ci][3]
                pt, r_pt, qlo = sinfo[idx]
                if j == 0:
                    ob[h] = Ob.next()
                o, r_o = ob[h]
                first, last = (j == 0), (j == n - 1)
                P.op("pe", lambda e: e.matmul(o[:, qlo:TQ], lhsT=va[:, jl, :], rhs=pt[:, qlo:TQ],
                                              start=first, stop=last), reads=[r_va, r_pt], writes=[r_o])
                if last:
                    rc, r_rc = gtmps[1]
                    P.op("dve", lambda e: e.reciprocal(out=rc[64:128, :], in_=o[64:128, :]), reads=[r_o], writes=[r_rc])
                    hf = (h % 2) * 64
                    P.op("dve", lambda e: e.tensor_tensor(out=attT[hf:hf + 64, h // 2, :], in0=o[0:64, :], in1=rc[64:128, :],
                                                          op=ALU.mult), reads=[r_o, r_rc], writes=[r_attT])

            gstate = {"gen": gen, "wait": 0}

            def tick():
                if gstate["gen"] is None:
                    return
                if gstate["wait"] > 0:
                    gstate["wait"] -= 1
                    return
                try:
                    gstate["wait"] = next(gstate["gen"]) or 0
                except StopIteration:
                    gstate["gen"] = None

            for idx in range(NI + 2):
                if idx < NI:
                    qk(idx)
                if idx >= 2:
                    pv(idx - 2)
                    if last_item_of_chunk[items[idx - 2][3]] == idx - 2:
                        issue_slab()
                tick()
            advance(gstate["gen"], 100000)

        def post(T):
            sgT, r_sgT = sgTs[T % 2]
            attT, r_attT = attTs[T % 2]
            for b in range(4):
                xo_, r_xo = xo.next()
                P.dma("sp", f"ld_xo{(xo.i - 1) % 2}", lambda e, b=b, xo_=xo_: e.dma_start(
                    out=xo_[:], in_=x_d[T * TQ + b * 128:T * TQ + (b + 1) * 128, :]), writes=[r_xo])
                yield 4
                pas = []
                for c in range(2):
                    pa, r_pa = Ab.next()
                    fns = [lambda e, hp=hp, b=b, c=c, pa=pa: e.matmul(
                        pa[:], lhsT=attT[:, hp, b * 128:(b + 1) * 128], rhs=woa[:, hp, c * 512:(c + 1) * 512],
                        start=(hp == 0), stop=False) for hp in range(4)]
                    fns += [lambda e, gp=gp, b=b, c=c, pa=pa: e.matmul(
                        pa[:], lhsT=sgT[:, gp, b * 128:(b + 1) * 128], rhs=wos[:, gp, c * 512:(c + 1) * 512],
                        start=False, stop=(gp == 3)) for gp in range(4)]
                    P.group("pe", fns, reads=[r_attT, r_sgT, r_woa, r_wos], writes=[r_pa])
                    pas.append((c, pa, r_pa))
                yield 5
                for c, pa, r_pa in pas:
                    P.op("dve", lambda e, c=c, pa=pa, xo_=xo_: e.tensor_tensor(
                        out=xo_[:, c * 512:(c + 1) * 512], in0=pa[:], in1=xo_[:, c * 512:(c + 1) * 512], op=ALU.add),
                        reads=[r_pa], writes=[r_xo])
                yield 2
                h1_tokens.append(P.dma("sp", f"st_h1{(xo.i - 1) % 2}", lambda e, b=b, xo_=xo_: e.dma_start(
                    out=out_d[T * TQ + b * 128:T * TQ + (b + 1) * 128, :], in_=xo_[:]), reads=[r_xo]))

        g0 = pre(0)
        advance(g0, 1000)
        r_wupd, r_wdnd = Res(), Res()
        for kc in range(KC):
            P.dma("pool", "w_upd", lambda e, kc=kc: e.dma_start(out=wupb_d[kc * 128:(kc + 1) * 128, :],
                                                                 in_=wup_d[kc * 128:(kc + 1) * 128, :],
                                                                 max_dma_last_dim=4096), writes=[r_wupd], extra=[kw_toks[0]])
        for i in range(0, NPAIR, 2):
            P.dma("pool", "w_dnd", lambda e, i=i: e.dma_start(out=wdnb_d[i * 128:(i + 2) * 128, :],
                                                               in_=wdn_d[i * 128:(i + 2) * 128, :],
                                                               max_dma_last_dim=4096), writes=[r_wdnd], extra=[kw_toks[0]])


        def chain(gens):
            for g_ in gens:
                for v_ in g_:
                    yield v_

        state = att_setup(0)
        pending_post = None
        for T in range(NT):
            gens = []
            if pending_post is not None:
                gens.append(pending_post)
            if T + 1 < NT:
                gens.append(pre(T + 1))
            attention(T, chain(gens) if gens else None, state)
            if T + 1 < NT:
                state = att_setup(T + 1)
            pending_post = post(T)
        advance(pending_post, 100000)

        wtoks = [r_wupd.w, r_wdnd.w]
        P.emit(final_waits=h1_tokens[-4:] + wtoks, gate_inc=gate)

    if only_pass1:
        gate_cm.__exit__(None, None, None)
        return nc
    with ExitStack() as es:
        def sb(name, shape, dt):
            return es.enter_context(nc.sbuf_tensor(name, shape, dt))

        def ps(name, shape, dt):
            return es.enter_context(nc.psum_tensor(name, shape, dt))

        P = Prog(nc)
        wup = sb("wup", [128, KC, 2 * DFF], BF16); r_wupq = [Res() for _ in range(4)]
        wdn = sb("wdn", [128, NPAIR, D], BF16); r_wdn = Res()
        xt = sb("xt2", [128, 5, D], F32); r_xt = [Res() for _ in range(5)]
        xnb = Rot([sb(f"xnb2{i}", [128, D], BF16) for i in range(2)])
        xnT = sb("xnT2", [128, KC, TQ], BF16); r_xnT = Res()
        mT = sb("mT", [128, NPAIR, TQ], BF16); r_mT = [Res() for _ in range(NPAIR)]
        ya = Rot([sb(f"ya{i}", [128, TQ], F32) for i in range(3)])
        yg = Rot([sb(f"yg{i}", [128, TQ], F32) for i in range(3)])
        sgl = Rot([sb(f"sgl{i}", [128, TQ], F32) for i in range(1)])
        carry = sb("carry", [128, 2, 2 * NPAIR, 2], F32); r_cy = [[Res() for _ in range(2 * NPAIR)] for _ in range(2)]
        cwt = sb("cwt", [128, 2 * NPAIR, 4], F32); r_cwt = Res()
        g2b = sb("g2b", [128, KC], F32); r_g2b = Res()
        g3b = sb("g3b", [128, D], F32); r_g3b = Res()
        idn = sb("idn2", [128, 128], BF16); r_idn = Res()
        onesf = sb("onesf2", [128, 128], F32); r_onesf = Res()
        cst = sb("cst2", [128, 2], F32); r_cst = Res()
        stt = (sb("ssq2", [128, 4], F32), sb("lnv2", [128, 4], F32), sb("rstd2", [128, 4], F32), [Res() for _ in range(4)])
        st3 = (sb("ssq3", [128, 4], F32), sb("lnv3", [128, 4], F32), sb("rstd3", [128, 4], F32), [Res() for _ in range(4)])
        trb = ps("trb2", [128, 1024], BF16); r_trb = Res()
        pbs = [ps(f"pc{i}", [128, 512], F32) for i in range(7)]
        Ab = Rot(pbs[0:5])
        Db = Rot(pbs[5:7])

        P.op("pool", lambda e: e.memset(onesf[:], 1.0), writes=[r_onesf])
        P.op("pool", lambda e: e.memset(cst[:, 0:1], EPS), writes=[r_cst])
        P.op("pool", lambda e: e.memset(carry[:], 0.0), writes=r_cy[0] + r_cy[1])
        P.op("pool", lambda e: e.affine_select(out=idn[:], in_=onesf[:], pattern=[[-1, 128]], compare_op=ALU.is_equal,
                                               fill=0.0, base=0, channel_multiplier=1), reads=[r_onesf], writes=[r_idn])
        P.dma("sp", "c_g2", lambda e: e.dma_start(out=g2b[:], in_=g2_d.rearrange("(k p) -> p k", p=128), allow_slow_non_contiguous=True), writes=[r_g2b])
        P.dma("sp", "c_g3", lambda e: e.dma_start(out=g3b[:], in_=g3_d.partition_broadcast(128)), writes=[r_g3b])
        for w in range(3):
            P.dma("sp", "c_cw", lambda e, w=w: e.dma_start(out=cwt[:, :, w], in_=cw_d[w].rearrange("(c p) -> p c", p=128),
                                                            allow_slow_non_contiguous=True), writes=[r_cwt])
        P.dma("sp", "c_cw", lambda e: e.dma_start(out=cwt[:, :, 3], in_=cb_d.rearrange("(c p) -> p c", p=128),
                                                  allow_slow_non_contiguous=True), writes=[r_cwt])
        wup_src = wupb_d.rearrange("(k p) n -> p k n", p=128)
        def load_wup(q):
            c0, c1 = q * 768, min((q + 1) * 768, DFF)
            for off in (0, DFF):
                P.dma("sp", f"w_up{q}", lambda e, c0=c0, c1=c1, off=off: e.dma_start(
                    out=wup[:, :, off + c0:off + c1], in_=wup_src[:, :, off + c0:off + c1]), writes=[r_wupq[q]])
        load_wup(0)

        out_toks = [None] * 4
        all_out = []
        ssq2, lnv2, rstd2, r_c2 = stt

        def norm_block(T, b):
            sl = (4 * T + b) % 5
            P.dma("sp", f"ld_h{sl}", lambda e: e.dma_start(
                out=xt[:, sl, :], in_=out_d[T * TQ + b * 128:T * TQ + (b + 1) * 128, :]), writes=[r_xt[sl]])
            rc_ = r_c2[b]
            xb, r_xb = xnb.next()
            P.op("act", lambda e: e.activation(out=xb[:], in_=xt[:, sl, :], func=AF.Square,
                                               accum_out=ssq2[:, b:b + 1]), reads=[r_xt[sl]], writes=[r_xb, rc_])
            P.op("act", lambda e: e.activation(out=lnv2[:, b:b + 1], in_=ssq2[:, b:b + 1], func=AF.Ln, bias=cst[:, 0:1],
                                               scale=1.0 / D), reads=[r_cst], writes=[rc_])
            P.op("act", lambda e: e.activation(out=rstd2[:, b:b + 1], in_=lnv2[:, b:b + 1], func=AF.Exp, scale=-0.5),
                 writes=[rc_])
            P.op("dve", lambda e: e.tensor_scalar(out=xb[:], in0=xt[:, sl, :], scalar1=rstd2[:, b:b + 1],
                                                  scalar2=None, op0=ALU.mult), reads=[r_xt[sl], rc_], writes=[r_xb])
            return xb, r_xb

        def transp_block(b, xb, r_xb):
            P.group("pe", [lambda e, kc=kc: e.transpose(trb[:, kc * 128:(kc + 1) * 128],
                                                        xb[:, kc * 128:(kc + 1) * 128], idn[:])
                           for kc in range(KC)], reads=[r_xb, r_idn], writes=[r_trb])
            P.op("dve", lambda e: e.tensor_tensor(out=xnT[:, :, b * 128:(b + 1) * 128],
                                                  in0=trb[:].rearrange("p (k t) -> p k t", t=128),
                                                  in1=g2b[:, :].unsqueeze(2).broadcast_to([128, KC, 128]), op=ALU.mult),
                 reads=[r_trb, r_g2b], writes=[r_xnT])

        for b in range(4):
            xb_, r_xb_ = norm_block(0, b)
            transp_block(b, xb_, r_xb_)
        for q in range(1, 4):
            load_wup(q)
        P.dma("sp", "w_dn", lambda e: e.dma_start(out=wdn[:], in_=wdnb_d.rearrange("(c p) n -> p c n", p=128)), writes=[r_wdn])
        for T in range(NT):
            pend = {}
            cur, nxt = T % 2, (T + 1) % 2
            prev_info = None
            prev2_info = None

            def finish(pinfo):
                ip, infos = pinfo
                (a_y, r_a), (g_y, r_g) = [(x[3], x[4]) for x in infos]
                sg, r_sg = sgl.next()
                P.op("act", lambda e: e.activation(out=sg[:], in_=g_y[:], func=AF.Silu), reads=[r_g], writes=[r_sg])
                P.op("pool", lambda e: e.tensor_tensor(out=mT[:, ip, :], in0=sg[:], in1=a_y[:], op=ALU.mult),
                     reads=[r_sg, r_a], writes=[r_mT[ip]])

            def fix1(x, cur=cur):
                ci, pa, r_pa, y, r_y = x
                P.op("dve", lambda e: e.scalar_tensor_tensor(
                    out=y[:, 0:2], in0=carry[:, cur, ci, 0:2], scalar=cwt[:, ci, 0:1], in1=y[:, 0:2], op0=ALU.mult, op1=ALU.add),
                    reads=[r_cy[cur][ci], r_cwt], writes=[r_y])

            def fix2(x, cur=cur):
                ci, pa, r_pa, y, r_y = x
                P.op("dve", lambda e: e.scalar_tensor_tensor(
                    out=y[:, 0:1], in0=carry[:, cur, ci, 1:2], scalar=cwt[:, ci, 1:2], in1=y[:, 0:1], op0=ALU.mult, op1=ALU.add),
                    reads=[r_cy[cur][ci], r_cwt], writes=[r_y])

            for i in range(NPAIR):
                info = []
                for typ in range(2):
                    ci = typ * NPAIR + i
                    col = ci * 128
                    pa, r_pa = Ab.next()
                    P.group("pe", [lambda e, kc=kc, col=col, pa=pa: e.matmul(
                        pa[:], lhsT=wup[:, kc, col:col + 128], rhs=xnT[:, kc, :],
                        start=(kc == 0), stop=(kc == KC - 1)) for kc in range(KC)], reads=[r_wupq[i // 6], r_xnT], writes=[r_pa])
                    y, r_y = (ya if typ == 0 else yg).next()
                    P.op("act", lambda e, ci=ci, pa=pa, y=y: e.activation(
                        out=y[:], in_=pa[:], func=AF.Identity, scale=cwt[:, ci, 2:3], bias=cwt[:, ci, 3:4]),
                        reads=[r_pa, r_cwt], writes=[r_y])
                    info.append((ci, pa, r_pa, y, r_y))
                if prev2_info is not None:
                    finish(prev2_info)
                    prev2_info = None
                pv_ = prev_info[1] if prev_info is not None else None
                for k_, (ci, pa, r_pa, y, r_y) in enumerate(info):
                    P.op("dve", lambda e, ci=ci, pa=pa, y=y: e.scalar_tensor_tensor(
                        out=y[:, 1:TQ], in0=pa[:, 0:TQ - 1], scalar=cwt[:, ci, 1:2], in1=y[:, 1:TQ], op0=ALU.mult, op1=ALU.add),
                        reads=[r_pa, r_cwt], writes=[r_y])
                    if pv_ is not None:
                        fix1(pv_[k_])
                for k_, (ci, pa, r_pa, y, r_y) in enumerate(info):
                    P.op("dve", lambda e, ci=ci, pa=pa, y=y: e.scalar_tensor_tensor(
                        out=y[:, 2:TQ], in0=pa[:, 0:TQ - 2], scalar=cwt[:, ci, 0:1], in1=y[:, 2:TQ], op0=ALU.mult, op1=ALU.add),
                        reads=[r_pa, r_cwt], writes=[r_y])
                    if pv_ is not None:
                        fix2(pv_[k_])
                for (ci, pa, r_pa, y, r_y) in info:
                    P.op("dve", lambda e, ci=ci, pa=pa, nxt=nxt: e.tensor_copy(out=carry[:, nxt, ci, :], in_=pa[:, TQ - 2:TQ]),
                         reads=[r_pa], writes=[r_cy[nxt][ci]])
                prev2_info = prev_info
                prev_info = (i, info)
            if prev2_info is not None:
                finish(prev2_info)
            for x in prev_info[1]:
                fix1(x)
            for x in prev_info[1]:
                fix2(x)
            finish(prev_info)
            if T + 1 < NT:
                pend[0] = norm_block(T + 1, 0)
            jk, r_jk = yg.items[0]
            jkb = jk[:].bitcast(BF16)
            for b in range(4):
                sl = (4 * T + b) % 5
                dbanks = [Db.next(), Db.next()]
                for c in range(2):
                    pa, r_pa = dbanks[c]
                    P.group("pe", [lambda e, i=i, b=b, c=c, pa=pa: e.matmul(
                        pa[:], lhsT=mT[:, i, b * 128:(b + 1) * 128], rhs=wdn[:, i, c * 512:(c + 1) * 512],
                        start=(i == 0), stop=False) for i in range(16)], reads=r_mT[0:16] + [r_wdn], writes=[r_pa])
                for c in range(2):
                    pa, r_pa = dbanks[c]
                    P.group("pe", [lambda e, i=i, b=b, c=c, pa=pa: e.matmul(
                        pa[:], lhsT=mT[:, i, b * 128:(b + 1) * 128], rhs=wdn[:, i, c * 512:(c + 1) * 512],
                        start=False, stop=(i == NPAIR - 1)) for i in range(16, NPAIR)], reads=r_mT[16:] + [r_wdn], writes=[r_pa])
                    P.op("dve", lambda e, sl=sl, c=c, pa=pa: e.tensor_tensor(
                        out=xt[:, sl, c * 512:(c + 1) * 512], in0=pa[:], in1=xt[:, sl, c * 512:(c + 1) * 512], op=ALU.add),
                        reads=[r_pa], writes=[r_xt[sl]])
                ssq, lnv, rstd, r_c3 = st3
                rc_ = r_c3[b]
                P.op("act", lambda e, b=b, sl=sl: e.activation(out=jkb, in_=xt[:, sl, :], func=AF.Square,
                                                              accum_out=ssq[:, b:b + 1]), reads=[r_xt[sl]], writes=[r_jk, rc_])
                P.op("act", lambda e, b=b: e.activation(out=lnv[:, b:b + 1], in_=ssq[:, b:b + 1], func=AF.Ln, bias=cst[:, 0:1],
                                                        scale=1.0 / D), reads=[r_cst], writes=[rc_])
                P.op("act", lambda e, b=b: e.activation(out=rstd[:, b:b + 1], in_=lnv[:, b:b + 1], func=AF.Exp, scale=-0.5),
                     writes=[rc_])
                P.op("dve", lambda e, b=b, sl=sl: e.scalar_tensor_tensor(out=xt[:, sl, :], in0=xt[:, sl, :], scalar=rstd[:, b:b + 1],
                                                                        in1=g3b[:], op0=ALU.mult, op1=ALU.mult),
                     reads=[rc_, r_g3b], writes=[r_xt[sl]])
                out_toks[b] = P.dma("sp", f"st_o{sl}", lambda e, T=T, b=b, sl=sl: e.dma_start(
                    out=out_d[T * TQ + b * 128:T * TQ + (b + 1) * 128, :], in_=xt[:, sl, :]), reads=[r_xt[sl]])
                all_out.append(out_toks[b])
                if T + 1 < NT:
                    if b + 1 < 4:
                        pend[b + 1] = norm_block(T + 1, b + 1)
                    transp_block(b, *pend[b])
        P.emit(final_waits=all_out[-8:], gate_wait=(gate, 1))
    gate_cm.__exit__(None, None, None)
    return nc


_NC_CACHE = {}


def _get_nc(S):
    if S not in _NC_CACHE:
        _NC_CACHE[S] = build_nc(S)
    return _NC_CACHE[S]


def make_in_maps(inputs, B, S):
    f = lambda a: np.ascontiguousarray(np.asarray(a, dtype=np.float32))
    shared = {
        "norm_mix_g": f(inputs["norm_mix_g"]).reshape(D),
        "w_in": f(inputs["w_in"]).reshape(D, IN_COLS),
        "b_forget": f(inputs["b_forget"]).reshape(H),
        "gmlp_norm_g": f(inputs["gmlp_norm_g"]).reshape(512),
        "w_spatial": f(inputs["w_spatial"]).reshape(8, 128, 128),
        "b_spatial": f(inputs["b_spatial"]).reshape(8, 128),
        "w_out": f(inputs["w_out"]).reshape(D, D),
        "norm_ffn_g": f(inputs["norm_ffn_g"]).reshape(D),
        "w_up": f(inputs["w_up"]).reshape(D, 2 * DFF),
        "conv_w": f(inputs["conv_w"]).reshape(3, 2 * DFF),
        "conv_b": f(inputs["conv_b"]).reshape(2 * DFF),
        "w_down": f(inputs["w_down"]).reshape(DFF, D),
        "norm_final_g": f(inputs["norm_final_g"]).reshape(D),
    }
    x = f(inputs["x"])
    return [dict(shared, x=np.ascontiguousarray(x[b])) for b in range(B)]


def kernel(**inputs):
    x = np.asarray(inputs["x"])
    B, S, _ = x.shape
    nc = _get_nc(S)
    in_maps = make_in_maps(inputs, B, S)
    res = run_bass_kernel_spmd(nc, in_maps, core_ids=list(range(B)))
    return np.stack([np.asarray(r["out"]) for r in res.results], axis=0).astype(np.float32)
```
